# Optimizing a Trainium2 kernel written in Bass

```python
import jax, jax.numpy as jnp
from jax import lax
import numpy as np

D_MODEL = 1024
BATCH = 32
SEQ = 2048
DEPTH = 1
DEC_BATCH = 8
DEC_SEQ = 64
PAST_LEN = 1024

CHUNK = 64
N_HEADS_A = 16
N_KV_A = 4
HEAD_DIM_A = 64
GROUP_A = N_HEADS_A // N_KV_A
WINDOW = 128
WINDOW_CHUNKS = WINDOW // CHUNK
BAND = (WINDOW_CHUNKS + 1) * CHUNK
ROT_DIM = HEAD_DIM_A // 4
ROPE_THETA = 500000.0
WIDTH_A = N_HEADS_A * HEAD_DIM_A
KV_WIDTH_A = N_KV_A * HEAD_DIM_A
N_HEADS_B = 8
HEAD_DK_B = 128
HEAD_DV_B = 128
CONV_W = 4
QK_WIDTH_B = N_HEADS_B * HEAD_DK_B
WIDTH_B = N_HEADS_B * HEAD_DV_B
CONV_CH = 2 * QK_WIDTH_B + WIDTH_B
SPLITS = (WIDTH_A, KV_WIDTH_A, KV_WIDTH_A, WIDTH_A, CONV_CH, N_HEADS_B, N_HEADS_B, WIDTH_B, D_MODEL, D_MODEL)
D_IN = sum(SPLITS)
ALPHA = (2.0 * DEPTH) ** 0.25
BETA_INIT = (8.0 * DEPTH) ** -0.25
LN_EPS = 1e-5
RMS_EPS = 1e-6
L2_EPS = 1e-6
NEG_INF = -1e30

kernel_name = "hybrid_swa_sink_gated_deltanet_stream_step"


def in_projection(x, w_in):
    y = jnp.einsum("bsd,de->bse", x, w_in)
    idx = [int(i) for i in np.cumsum(SPLITS)[:-1]]
    return jnp.split(y, idx, axis=-1)


def partial_rope(x, pos):
    half = ROT_DIM // 2
    inv_freq = jnp.power(ROPE_THETA, -jnp.arange(half, dtype=jnp.float32) / half)
    ang = pos[:, None] * inv_freq[None, :]
    cos = jnp.cos(ang)[None, :, None, :]
    sin = jnp.sin(ang)[None, :, None, :]
    xf = x.astype(jnp.float32)
    x1 = xf[..., :half]
    x2 = xf[..., half:ROT_DIM]
    rot = jnp.concatenate([x1 * cos - x2 * sin, x2 * cos + x1 * sin], axis=-1)
    return jnp.concatenate([rot.astype(x.dtype), x[..., ROT_DIM:]], axis=-1)


def sink_softmax(s, sinks):
    m = jnp.maximum(jnp.max(s, axis=-1, keepdims=True), sinks)
    p = jnp.exp(s - m)
    return p / (jnp.sum(p, axis=-1, keepdims=True) + jnp.exp(sinks - m))


def window_attention_prompt(q, k, v, attn_sinks):
    bsz, s_len = q.shape[:2]
    n_c = s_len // CHUNK
    pad = WINDOW_CHUNKS * CHUNK
    kp = jnp.pad(k, ((0, 0), (pad, 0), (0, 0), (0, 0))).reshape(bsz, n_c + WINDOW_CHUNKS, CHUNK, N_KV_A, HEAD_DIM_A)
    vp = jnp.pad(v, ((0, 0), (pad, 0), (0, 0), (0, 0))).reshape(bsz, n_c + WINDOW_CHUNKS, CHUNK, N_KV_A, HEAD_DIM_A)
    kb = jnp.concatenate([kp[:, j:j + n_c] for j in range(WINDOW_CHUNKS + 1)], axis=2)
    vb = jnp.concatenate([vp[:, j:j + n_c] for j in range(WINDOW_CHUNKS + 1)], axis=2)
    qc = q.reshape(bsz, n_c, CHUNK, N_KV_A, GROUP_A, HEAD_DIM_A)
    s = jnp.einsum("bcqkgd,bcjkd->bckgqj", qc, kb).astype(jnp.float32) * (HEAD_DIM_A ** -0.5)
    kpos = jnp.arange(n_c)[:, None] * CHUNK - pad + jnp.arange(BAND)[None, :]
    s = jnp.where((kpos >= 0)[None, :, None, None, None, :], s, NEG_INF)
    sinks = attn_sinks.astype(jnp.float32).reshape(N_KV_A, GROUP_A)[None, None, :, :, None, None]
    p = sink_softmax(s, sinks)
    o = jnp.einsum("bckgqj,bcjkd->bcqkgd", p.astype(v.dtype), vb)
    return o.reshape(bsz, s_len, WIDTH_A)


def window_attention_sample(q, k_all, v_all, attn_sinks):
    bsz, l_len = q.shape[:2]
    qg = q.reshape(bsz, l_len, N_KV_A, GROUP_A, HEAD_DIM_A)
    s = jnp.einsum("bqkgd,bjkd->bkgqj", qg, k_all).astype(jnp.float32) * (HEAD_DIM_A ** -0.5)
    sinks = attn_sinks.astype(jnp.float32).reshape(N_KV_A, GROUP_A)[None, :, :, None, None]
    p = sink_softmax(s, sinks)
    o = jnp.einsum("bkgqj,bjkd->bqkgd", p.astype(v_all.dtype), v_all)
    return o.reshape(bsz, l_len, WIDTH_A)


def causal_conv(u, hist, conv_w):
    ext = jnp.concatenate([hist, u], axis=1)
    out = lax.conv_general_dilated(ext, conv_w[:, None, :].astype(ext.dtype), window_strides=(1,), padding="VALID",
                                   dimension_numbers=("NWC", "WIO", "NWC"), feature_group_count=CONV_CH)
    return jax.nn.silu(out), ext[:, ext.shape[1] - (CONV_W - 1):]


def l2norm(t):
    return t * lax.rsqrt(jnp.sum(t * t, axis=-1, keepdims=True) + L2_EPS)


def gated_delta_chunked(q, k, v, g, beta, s0):
    bsz, nh, l_len, _ = q.shape
    dv = v.shape[-1]
    c = min(CHUNK, l_len)
    n = l_len // c

    def blocks(t):
        return t.reshape((bsz, nh, n, c) + t.shape[3:])

    q, k, v, g, beta = blocks(q), blocks(k), blocks(v), blocks(g), blocks(beta)
    gc = jnp.cumsum(g, axis=-1)
    idx = jnp.arange(c)
    causal = idx[:, None] >= idx[None, :]
    strict = idx[:, None] > idx[None, :]
    diff = gc[..., :, None] - gc[..., None, :]
    decay = jnp.where(causal, jnp.exp(jnp.where(causal, diff, 0.0)), 0.0)
    k_beta = k * beta[..., None]
    m = jnp.where(strict, jnp.einsum("bhnid,bhnjd->bhnij", k_beta, k) * decay, 0.0)
    eye = jnp.eye(c, dtype=jnp.float32)
    t_inv = lax.linalg.triangular_solve(eye + m, jnp.broadcast_to(eye, m.shape), left_side=True, lower=True)
    u = jnp.einsum("bhnij,bhnje->bhnie", t_inv, v * beta[..., None])
    w = jnp.einsum("bhnij,bhnjd->bhnid", t_inv, k_beta * jnp.exp(gc)[..., None])
    qk = jnp.einsum("bhnid,bhnjd->bhnij", q, k) * decay
    q_dec = q * jnp.exp(gc)[..., None]
    k_dec = k * jnp.exp(gc[..., -1:] - gc)[..., None]
    g_last = jnp.exp(gc[..., -1])

    def step(s, blk):
        u_i, w_i, qk_i, q_i, k_i, gl_i = blk
        v_new = u_i - jnp.einsum("bhid,bhde->bhie", w_i, s)
        o_i = jnp.einsum("bhid,bhde->bhie", q_i, s) + jnp.einsum("bhij,bhje->bhie", qk_i, v_new)
        s = s * gl_i[..., None, None] + jnp.einsum("bhid,bhie->bhde", k_i, v_new)
        return s, o_i

    xs = tuple(jnp.moveaxis(t, 2, 0) for t in (u, w, qk, q_dec, k_dec, g_last))
    s_final, o = lax.scan(step, s0, xs)
    o = jnp.moveaxis(o, 0, 2).reshape(bsz, nh, l_len, dv)
    return o, s_final


def gated_rmsnorm(o, z, w):
    bsz, l_len = o.shape[:2]
    o = o * lax.rsqrt(jnp.mean(o * o, axis=-1, keepdims=True) + RMS_EPS) * w.astype(jnp.float32)
    return (o.reshape(bsz, l_len, WIDTH_B) * jax.nn.silu(z.astype(jnp.float32))).astype(z.dtype)


def layer_norm(x, g, b):
    xf = x.astype(jnp.float32)
    mu = jnp.mean(xf, axis=-1, keepdims=True)
    var = jnp.mean(jnp.square(xf - mu), axis=-1, keepdims=True)
    y = (xf - mu) * lax.rsqrt(var + LN_EPS) * g.astype(jnp.float32) + b.astype(jnp.float32)
    return y.astype(x.dtype)


def hybrid_layer(x, pos_offset, k_hist, v_hist, conv_hist, s0, w_in, attn_sinks, conv_w, a_log, dt_bias,
                 delta_norm_w, w_o_attn, w_o_delta, w_out, ln_g, ln_b):
    bsz, l_len, _ = x.shape
    qa, ka, va, za, qkv_b, a_b, b_b, z_b, g_a, g_b = in_projection(x, w_in)
    pos = jnp.arange(l_len, dtype=jnp.float32) + pos_offset
    qa = partial_rope(qa.reshape(bsz, l_len, N_HEADS_A, HEAD_DIM_A), pos)
    ka = partial_rope(ka.reshape(bsz, l_len, N_KV_A, HEAD_DIM_A), pos)
    va = va.reshape(bsz, l_len, N_KV_A, HEAD_DIM_A)
    if k_hist is None:
        oa = window_attention_prompt(qa, ka, va, attn_sinks)
        keep = min(WINDOW, PAST_LEN)
        new_k = ka[:, l_len - keep:]
        new_v = va[:, l_len - keep:]
        conv_hist = jnp.zeros((bsz, CONV_W - 1, CONV_CH), dtype=x.dtype)
        s0 = jnp.zeros((bsz, N_HEADS_B, HEAD_DK_B, HEAD_DV_B), dtype=jnp.float32)
    else:
        k_all = jnp.concatenate([k_hist.astype(ka.dtype), ka], axis=1)
        v_all = jnp.concatenate([v_hist.astype(va.dtype), va], axis=1)
        oa = window_attention_sample(qa, k_all, v_all, attn_sinks)
        keep = k_hist.shape[1]
        new_k = k_all[:, k_all.shape[1] - keep:]
        new_v = v_all[:, v_all.shape[1] - keep:]
    y_a = jnp.einsum("bse,ed->bsd", oa * jax.nn.silu(za), w_o_attn)

    qkv_c, new_conv = causal_conv(qkv_b, conv_hist.astype(qkv_b.dtype), conv_w)
    qb, kb, vb = jnp.split(qkv_c.astype(jnp.float32), [QK_WIDTH_B, 2 * QK_WIDTH_B], axis=-1)
    qb = l2norm(qb.reshape(bsz, l_len, N_HEADS_B, HEAD_DK_B)) * (HEAD_DK_B ** -0.5)
    kb = l2norm(kb.reshape(bsz, l_len, N_HEADS_B, HEAD_DK_B))
    vb = vb.reshape(bsz, l_len, N_HEADS_B, HEAD_DV_B)
    g = -jnp.exp(a_log.astype(jnp.float32)) * jax.nn.softplus(a_b.astype(jnp.float32) + dt_bias.astype(jnp.float32))
    beta = jax.nn.sigmoid(b_b.astype(jnp.float32))
    ob, s_new = gated_delta_chunked(jnp.swapaxes(qb, 1, 2), jnp.swapaxes(kb, 1, 2), jnp.swapaxes(vb, 1, 2),
                                    jnp.swapaxes(g, 1, 2), jnp.swapaxes(beta, 1, 2), s0.astype(jnp.float32))
    ob = gated_rmsnorm(jnp.swapaxes(ob, 1, 2), z_b, delta_norm_w)
    y_b = jnp.einsum("bse,ed->bsd", ob.astype(x.dtype), w_o_delta)

    h = jax.nn.sigmoid(g_a) * y_a + jax.nn.sigmoid(g_b) * y_b
    sub = jnp.einsum("bsd,de->bse", h, w_out)
    y = layer_norm(ALPHA * x + sub, ln_g, ln_b)
    return y, new_k, new_v, new_conv.astype(x.dtype), s_new.astype(x.dtype)


def setup_inputs(seed: int = 0) -> dict:
    key = jax.random.key(seed)
    ks = jax.random.split(key, 20)
    keep = min(WINDOW, PAST_LEN)
    nrm = jax.random.normal
    dt = jnp.exp(jax.random.uniform(ks[10], (N_HEADS_B,), minval=float(np.log(1e-3)), maxval=float(np.log(1e-1))))
    return {
        "x_prompt": nrm(ks[0], (BATCH, SEQ, D_MODEL), jnp.float32),
        "x_sample": nrm(ks[1], (DEC_BATCH, DEC_SEQ, D_MODEL), jnp.float32),
        "cache_attn_k": nrm(ks[2], (DEC_BATCH, keep, N_KV_A, HEAD_DIM_A), jnp.float32),
        "cache_attn_v": nrm(ks[3], (DEC_BATCH, keep, N_KV_A, HEAD_DIM_A), jnp.float32),
        "state_conv": nrm(ks[4], (DEC_BATCH, CONV_W - 1, CONV_CH), jnp.float32),
        "state_delta": nrm(ks[5], (DEC_BATCH, N_HEADS_B, HEAD_DK_B, HEAD_DV_B), jnp.float32) * HEAD_DK_B ** -0.5,
        "w_in": nrm(ks[6], (D_MODEL, D_IN), jnp.float32) * D_MODEL ** -0.5,
        "attn_sinks": nrm(ks[7], (N_HEADS_A,), jnp.float32),
        "conv_w": nrm(ks[8], (CONV_W, CONV_CH), jnp.float32) * CONV_W ** -0.5,
        "a_log": jnp.log(jax.random.uniform(ks[9], (N_HEADS_B,), minval=1.0, maxval=16.0)),
        "dt_bias": dt + jnp.log(-jnp.expm1(-dt)),
        "delta_norm_w": 1.0 + 0.02 * nrm(ks[11], (HEAD_DV_B,), jnp.float32),
        "w_o_attn": nrm(ks[12], (WIDTH_A, D_MODEL), jnp.float32) * WIDTH_A ** -0.5 * BETA_INIT,
        "w_o_delta": nrm(ks[13], (WIDTH_B, D_MODEL), jnp.float32) * WIDTH_B ** -0.5 * BETA_INIT,
        "w_out": nrm(ks[14], (D_MODEL, D_MODEL), jnp.float32) * D_MODEL ** -0.5 * BETA_INIT,
        "ln_g": 1.0 + 0.02 * nrm(ks[15], (D_MODEL,), jnp.float32),
        "ln_b": 0.02 * nrm(ks[16], (D_MODEL,), jnp.float32),
    }


def reference(x_prompt, x_sample, cache_attn_k, cache_attn_v, state_conv, state_delta, w_in, attn_sinks, conv_w,
              a_log, dt_bias, delta_norm_w, w_o_attn, w_o_delta, w_out, ln_g, ln_b):
    yp, yp_new = x_prompt, x_sample
    kp = vp = cp = sp = ks = vs = cs = ss = None
    for _layer in range(DEPTH):
        yp, kp, vp, cp, sp = hybrid_layer(yp, 0.0, None, None, None, None, w_in, attn_sinks, conv_w, a_log,
                                          dt_bias, delta_norm_w, w_o_attn, w_o_delta, w_out, ln_g, ln_b)
        yp_new, ks, vs, cs, ss = hybrid_layer(yp_new, float(PAST_LEN), cache_attn_k, cache_attn_v, state_conv,
                                              state_delta, w_in, attn_sinks, conv_w, a_log, dt_bias, delta_norm_w,
                                              w_o_attn, w_o_delta, w_out, ln_g, ln_b)
    return (yp, yp_new, kp, vp, cp, sp, ks, vs, cs, ss)
```

```python
import numpy as np
from contextlib import ExitStack
import concourse.bass as bass
import concourse.mybir as mybir
from concourse.bass_utils import run_bass_kernel_spmd

F32 = mybir.dt.float32
BF16 = mybir.dt.bfloat16
F32R = mybir.dt.float32r
AF = mybir.ActivationFunctionType
ALU = mybir.AluOpType
AX = mybir.AxisListType

EPOCH = 30000
NDMA = 24

D = 1024
SEQ = 2048
NPS = 4
DEC = 64
PAST = 1024
T = 256
NG = 26
ALPHA = 2.0 ** 0.25
NEG = -30000.0
(G_K, G_VAB, G_Q0, G_Q1, G_ZA0, G_ZA1, G_OA0, G_OA1, G_GA0, G_GA1) = range(10)
G_D0 = 10
(G_ZB0, G_ZB1, G_GB0, G_GB1, G_OB0, G_OB1, G_OUT0, G_OUT1) = range(18, 26)
GCOLS = [512, 272, 512, 512, 512, 512, 512, 512, 512, 512] + [384] * 8 + [512] * 8


class Sched:
    ENG = ("pe", "act", "dve", "pool", "sp")

    def __init__(self, nc, stack):
        self.nc = nc
        self.stack = stack
        self.eng = {"pe": nc.tensor, "act": nc.scalar, "dve": nc.vector, "pool": nc.gpsimd, "sp": nc.sync}
        self.cnt = {e: 0 for e in self.ENG}
        self.sems = {e: [] for e in self.ENG}
        self.seen = {e: {} for e in self.ENG}
        self.lastw = {}
        self.readers = {}
        self.dma_sems = [stack.enter_context(nc.semaphore(f"dq{i}")) for i in range(NDMA)]
        self.dma_cnt = [0] * NDMA
        self.dma_rr = 0

    def _sem(self, e, n):
        idx = (n - 1) // EPOCH
        while len(self.sems[e]) <= idx:
            self.sems[e].append(self.stack.enter_context(
                self.nc.semaphore(f"s_{e}_{len(self.sems[e])}")))
        return self.sems[e][idx], (n - 1) % EPOCH + 1

    def _wait(self, e, ev):
        f, n = ev
        if self.seen[e].get(f, 0) >= n:
            return
        self.seen[e][f] = n
        if isinstance(f, tuple):
            self.eng[e].wait_ge(self.dma_sems[f[1]], 16 * n)
        else:
            sem, v = self._sem(f, n)
            self.eng[e].wait_ge(sem, v)

    def _deps(self, e, reads, writes):
        deps = set()
        for k in reads:
            w = self.lastw.get(k)
            if w is not None:
                deps.add(w)
        for k in writes:
            w = self.lastw.get(k)
            if w is not None:
                deps.add(w)
            for r in self.readers.get(k, ()):
                deps.add(r)
        for ev in sorted(deps, key=lambda x: (str(x[0]), x[1])):
            if ev[0] == e and e == "pe":
                continue
            self._wait(e, ev)

    def _commit(self, ev, reads, writes):
        for k in writes:
            self.lastw[k] = ev
            self.readers[k] = []
        for k in reads:
            self.readers.setdefault(k, []).append(ev)

    def op(self, e, fn, reads=(), writes=()):
        self._deps(e, reads, writes)
        self.cnt[e] += 1
        n = self.cnt[e]
        sem, _ = self._sem(e, n)
        fn(self.eng[e]).then_inc(sem, 1)
        ev = (e, n)
        self._commit(ev, reads, writes)
        return ev

    def dma(self, e, out, in_, reads=(), writes=(), **kw):
        self._deps(e, reads, writes)
        s = self.dma_rr
        self.dma_rr = (self.dma_rr + 1) % NDMA
        f = ("dma", s)
        if self.dma_cnt[s] > 0:
            self._wait(e, (f, self.dma_cnt[s]))
        self.dma_cnt[s] += 1
        n = self.dma_cnt[s]
        self.eng[e].dma_start(out=out, in_=in_, **kw).then_inc(self.dma_sems[s], 16)
        ev = (f, n)
        self._commit(ev, reads, writes)
        return ev

    def emit(self):
        for s in range(NDMA):
            if self.dma_cnt[s] > 0:
                self._wait("sp", (("dma", s), self.dma_cnt[s]))
        for e in self.ENG:
            if e != "sp" and self.cnt[e] > 0:
                self._wait("sp", (e, self.cnt[e]))


def _host_consts():
    c = {}
    c["ident"] = np.eye(128, dtype=np.float32)
    m = np.arange(128)
    d = m % 64
    pos = np.concatenate([np.arange(SEQ), PAST + np.arange(DEC)]).astype(np.float64)
    inv = 500000.0 ** (-np.arange(8) / 8.0)
    ang = pos[None, :] * inv[d % 8][:, None]
    rc = np.where((d < 16)[:, None], np.cos(ang), 1.0)
    rs = np.where((d < 8)[:, None], -np.sin(ang), np.where((d < 16)[:, None], np.sin(ang), 0.0))
    c["ropec"] = rc.astype(np.float32)
    c["ropes"] = rs.astype(np.float32)
    ps = np.zeros((128, 128), np.float32)
    for mm in range(128):
        if mm % 64 < 8:
            ps[mm + 8, mm] = 1.0
        elif mm % 64 < 16:
            ps[mm - 8, mm] = 1.0
    c["pswap"] = ps
    ml = np.zeros((3, 128), np.float32)
    ml[0, :64] = 1; ml[1, 64:] = 1; ml[2, :] = 1
    mg = np.zeros((3, 256), np.float32)
    mg[0, 192:] = NEG; mg[1, :64] = NEG
    mf = mg.copy(); mf[2, :128] = NEG
    c["amask"] = np.concatenate([ml, mg, mf], axis=1)
    same = (m[:, None] // 64) == (m[None, :] // 64)
    tri = (same & (m[:, None] <= m[None, :])).astype(np.float32)
    bo = same.astype(np.float32)
    inda = np.repeat((m < 64).astype(np.float32)[:, None], 128, 1)
    indb = np.repeat((m >= 64).astype(np.float32)[:, None], 128, 1)
    ones = np.ones((128, 128), np.float32)
    mpos = np.where(same & (m[None, :] >= m[:, None]), 0.0, 1e4).astype(np.float32)
    offd = (1.0 - np.eye(128)).astype(np.float32)
    c["dmats"] = np.stack([tri, bo, inda, indb, ones, offd, np.eye(128, dtype=np.float32)], 1)
    c["mpos4"] = np.tile(mpos, (1, 4)).astype(np.float32)
    return c


def _wall(w_in, w_o_attn, w_o_delta, w_out):
    cols = []
    kA = w_in[:, 1024:1280]; vA = w_in[:, 1280:1536]
    kd = np.concatenate([np.concatenate([kA[:, g * 64:(g + 1) * 64]] * 2, 1) for g in range(4)], 1)
    cols.append(kd)
    cols.append(np.concatenate([vA, w_in[:, 5632:5648]], 1))
    cols.append(w_in[:, 0:512]); cols.append(w_in[:, 512:1024])
    cols.append(w_in[:, 1536:2048]); cols.append(w_in[:, 2048:2560])
    cols.append(w_o_attn[:, 0:512]); cols.append(w_o_attn[:, 512:1024])
    cols.append(w_in[:, 6672:7184]); cols.append(w_in[:, 7184:7696])
    for h in range(8):
        cols.append(np.concatenate([w_in[:, 2560 + h * 128:2560 + (h + 1) * 128],
                                    w_in[:, 3584 + h * 128:3584 + (h + 1) * 128],
                                    w_in[:, 4608 + h * 128:4608 + (h + 1) * 128]], 1))
    cols.append(w_in[:, 5648:6160]); cols.append(w_in[:, 6160:6672])
    cols.append(w_in[:, 7696:8208]); cols.append(w_in[:, 8208:8720])
    cols.append(w_o_delta[:, 0:512]); cols.append(w_o_delta[:, 512:1024])
    cols.append(w_out[:, 0:512]); cols.append(w_out[:, 512:1024])
    wall = np.zeros((NG, 128, 8, 512), np.float32)
    for g, cm in enumerate(cols):
        assert cm.shape[1] == GCOLS[g]
        wall[g, :, :, :cm.shape[1]] = cm.reshape(8, 128, cm.shape[1]).transpose(1, 0, 2)
    return wall


DBG = {"stage": 99}
def build_program(nps=NPS, with_sample=True):
    nc = bass.Bass("TRN2", target_bir_lowering=False)

    def din(name, shape):
        return nc.dram_tensor(name, list(shape), F32, kind="ExternalInput").ap()

    def dout(name, shape):
        return nc.dram_tensor(name, list(shape), F32, kind="ExternalOutput").ap()

    xp = din("xp", [nps, SEQ, D]); xs = din("xs", [DEC, D])
    ckd = din("ckd", [128, 512]); ck = din("ck", [128, 256]); cv = din("cv", [128, 256])
    sconv = din("sconv", [128, 3, 24]); sdelta = din("sdelta", [8, 128, 128])
    wall = din("wall", [NG, 128, 8, 512])
    sinks_d = din("sinks", [16]); convw_d = din("convw", [128, 24, 4])
    alog_d = din("alog", [8]); dtb_d = din("dtb", [8]); normw_d = din("normw", [128, 1])
    lng_d = din("lng", [D]); lnb_d = din("lnb", [D])
    ident_d = din("ident", [128, 128]); ropec_d = din("ropec", [128, SEQ + DEC]); ropes_d = din("ropes", [128, SEQ + DEC])
    pswap_d = din("pswap", [128, 128]); amask_d = din("amask", [3, 640])
    dmats_d = din("dmats", [128, 7, 128]); mpos4_d = din("mpos4", [128, 512])

    yp = dout("yp", [nps, SEQ, D]); ys = dout("ys", [DEC, D])
    nkp = dout("nkp", [nps, 128, 256]); nvp = dout("nvp", [nps, 128, 256])
    ncp = dout("ncp", [nps, 72, 128]); ndp = dout("ndp", [nps, 8, 128, 128])
    nks = dout("nks", [128, 256]); nvs = dout("nvs", [128, 256])
    ncs = dout("ncs", [72, 128]); nds = dout("nds", [8, 128, 128])
    wscr = nc.dram_tensor("wscr", [NG, 128, 8, 512], BF16).ap()
    dgscr = nc.dram_tensor("dgscr", [8, 128, 12, 128], BF16).ap()

    with ExitStack() as st:
        S = Sched(nc, st)

        def sb(name, shape, dt=F32):
            return st.enter_context(nc.sbuf_tensor(name, list(shape), dt))

        def psum(name, shape, dt=F32):
            return st.enter_context(nc.psum_tensor(name, list(shape), dt))

        pmb = [psum(f"pm{i}", [128, 512]) for i in range(2)]
        ptb = psum("ptb", [128, 1024], BF16)
        pO = psum("pO", [128, 512])
        pw = [psum(f"pw{i}", [128, 512]) for i in range(4)]
        rr = {"pm": 0, "pmlim": 2, "pw": 0, "s0": 0, "s1": 0}

        PMDEEP = [("pm", 0), ("pm", 1), ("pw", 0), ("pw", 1), ("pw", 2)]

        def next_pm():
            lst = PMDEEP[:rr["pmlim"]]
            i = rr["pm"] % len(lst); rr["pm"] = i + 1
            nm, j = lst[i]
            bank = pmb[j] if nm == "pm" else pw[j]
            return bank[:, 0:256], (nm, j)

        def next_pm_full():
            lst = PMDEEP[:rr["pmlim"]]
            i = rr["pm"] % len(lst); rr["pm"] = i + 1
            nm, j = lst[i]
            return (pmb[j] if nm == "pm" else pw[j]), (nm, j)

        def next_pw():
            if rr["pmlim"] == 5:
                return pw[3], ("pw", 3)
            i = rr["pw"]; rr["pw"] = (i + 1) % 4
            return pw[i], ("pw", i)

        def next_ps(s):
            k = "s%d" % s
            i = 2 * s + rr[k]; rr[k] = 1 - rr[k]
            return pw[i], ("pw", i)

        ident_f = sb("ident_f", [128, 128]); ident_b = sb("ident_b", [128, 128], BF16)
        pswap_b = sb("pswap_b", [128, 128], BF16); pswap_f = sb("pswap_f", [128, 128])
        amask_f = sb("amask_f", [3, 640]); amask_b = sb("amask_b", [3, 640], BF16)
        dm = sb("dm", [128, 7, 128]); mpos4 = sb("mpos4s", [128, 512])
        ones_b = sb("ones_b", [128, 128], BF16)
        sinks = sb("sinks_s", [128, 16]); nsinks = sb("nsinks", [128, 16])
        convw = sb("convw_s", [128, 24, 4]); alog = sb("alog_s", [128, 8]); nega = sb("nega", [128, 8])
        dtb = sb("dtb_s", [128, 8]); normw = sb("normw_s", [128, 1]); nw2 = sb("nw2", [128, 1])
        lng = sb("lng_s", [128, D]); lnb = sb("lnb_s", [128, D])
        ropec = sb("ropec_s", [128, T]); ropes = sb("ropes_s", [128, T])
        TRI, BO, INDA, INDB, ONESF, OFFD, IDF = range(7)

        S.dma("sp", ident_f[:], ident_d, writes=["ident_f"])
        S.dma("sp", pswap_f[:], pswap_d, writes=["pswap_f"])
        S.dma("sp", amask_f[:], amask_d, writes=["amask_f"])
        S.dma("sp", dm[:], dmats_d, writes=["dm"])
        S.dma("sp", mpos4[:], mpos4_d, writes=["mpos4"])
        S.dma("sp", sinks[:], sinks_d.partition_broadcast(128), writes=["sinks"])
        S.dma("sp", convw[:], convw_d, writes=["convw"])
        S.dma("sp", alog[:], alog_d.partition_broadcast(128), writes=["alog"])
        S.dma("sp", dtb[:], dtb_d.partition_broadcast(128), writes=["dtb"])
        S.dma("sp", normw[:], normw_d, writes=["normw"])
        S.dma("sp", lng[:], lng_d.partition_broadcast(128), writes=["lng"])
        S.dma("sp", lnb[:], lnb_d.partition_broadcast(128), writes=["lnb"])
        S.op("dve", lambda e: e.tensor_copy(out=ident_b[:], in_=ident_f[:]), reads=["ident_f"], writes=["ident_b"])
        S.op("dve", lambda e: e.tensor_copy(out=pswap_b[:], in_=pswap_f[:]), reads=["pswap_f"], writes=["pswap_b"])
        S.op("dve", lambda e: e.tensor_copy(out=amask_b[:], in_=amask_f[:]), reads=["amask_f"], writes=["amask_b"])
        S.op("dve", lambda e: e.tensor_copy(out=ones_b[:], in_=dm[:, ONESF, :]), reads=["dm"], writes=["ones_b"])
        S.op("dve", lambda e: e.tensor_scalar(out=nsinks[:], in0=sinks[:], scalar1=-1.0, scalar2=None, op0=ALU.mult),
             reads=["sinks"], writes=["nsinks"])
        S.op("act", lambda e: e.activation(out=nega[:], in_=alog[:], func=AF.Exp), reads=["alog"], writes=["nega"])
        S.op("dve", lambda e: e.tensor_scalar(out=nega[:], in0=nega[:], scalar1=-1.0, scalar2=None, op0=ALU.mult),
             reads=["nega"], writes=["nega"])
        S.op("dve", lambda e: e.tensor_scalar(out=nw2[:], in0=normw[:], scalar1=float(np.sqrt(128.0)), scalar2=None, op0=ALU.mult),
             reads=["normw"], writes=["nw2"])

        xin = sb("xin", [128, D]); xbf = sb("xbf", [128, D], BF16)
        xT = sb("xT", [128, 8, T], BF16)
        QT = sb("QT", [128, 8, T], BF16)
        Kwin = sb("Kwin", [128, 4, 128 + T], BF16)
        kf = sb("kf", [128, 4, 128])
        Vwin = sb("Vwin", [128, 1 + T // 128, 4, 2, 128], BF16)
        vf = sb("vf", [128, 256])
        abt = sb("abt", [128, T // 128, 16])
        gtok = sb("gtok", [128, T // 128, 8]); btok = sb("btok", [128, T // 128, 8])
        tmpab = sb("tmpab", [128, T // 128, 8])
        bufZ = sb("bufZ", [128, 8, T], BF16); bufGate = sb("bufGate", [128, 8, T], BF16)
        bufG = sb("bufG", [128, 8, T], BF16); hA = sb("hA", [128, 8, T], BF16)
        tmpb = sb("tmpb", [128, T], BF16); tmpf1 = sb("tmpf1", [128, T]); tmpf2 = sb("tmpf2", [128, T])
        xres = sb("xres", [128, D]); yt = sb("yt", [128, D]); ysq = sb("ysq", [128, D])
        lnst = sb("lnst", [128, 8]); trf = sb("trf", [128, 128])
        ATT = []
        for s_ in range(2):
            ATT.append(dict(
                Pm=sb(f"Pm{s_}", [128, 4, 256], BF16), Pn=sb(f"Pn{s_}", [128, 4, 256], BF16), PT=sb(f"PT{s_}", [128, 8, 128], BF16),
                mx=sb(f"stmx{s_}", [128, 4]), ng=sb(f"stng{s_}", [128, 4]), sm=sb(f"stsm{s_}", [128, 4]),
                es=sb(f"stes{s_}", [128, 4]), rd=sb(f"strd{s_}", [128, 4])))
        ubs = [sb(f"ub{i}", [128, 3, 8 + T], BF16) for i in range(2)]
        dgb = [sb(f"dgb{i}", [128, 12, 128], BF16) for i in range(2)]
        dgbuild = dgb[0]
        hist = sb("hist", [128, 3, 24])
        rnf = sb("rnf", [128, 2, T])
        qTa = sb("qTa", [128, 8, T], BF16); kTa = sb("kTa", [128, 8, T], BF16); vTa = sb("vTa", [128, 8, T], BF16)
        Sf = sb("Sf", [128, 8, 128]); Sb = sb("Sb", [128, 8, 128], BF16)
        ktok = sb("ktok", [128, T // 128, 8, 128], BF16); vtok = sb("vtok", [128, T // 128, 8, 128], BF16)
        rhsb = sb("rhsb", [128, 8, 128], F32R)
        dmr = sb("dmr", [128, 2, 128], F32R); mposr = sb("mposr", [128, 512], F32R)
        dcT = sb("dcT", [128, 8, 128]); E1 = sb("E1", [128, 8, 128], BF16); bbs = sb("bbs", [128, 8, 128])
        gsm = sb("gsm", [128, 32]); egc = sb("egc", [128, 8]); edec = sb("edec", [128, 8]); ela = sb("ela", [128, 16])
        sc1 = sb("sc1", [128, 8]); ngc = sb("ngc", [128, 8])
        DEL = []
        for s_ in range(2):
            d_ = dict(
                kbT=sb(f"kbT{s_}", [128, 4, 128], BF16),
                Ab=[sb(f"Ab{s_}_{i}", [128, 4, 128], BF16) for i in range(2)],
                Nb=[sb(f"Nb{s_}_{i}", [128, 4, 128], BF16) for i in range(2)],
                Rf=sb(f"Rf{s_}", [128, 4, 128]), Rb=sb(f"Rb{s_}", [128, 4, 128], BF16),
                kbg=sb(f"kbg{s_}", [128, 4, 128], BF16), kdec=sb(f"kdec{s_}", [128, 4, 128], BF16),
                vbt=sb(f"vbt{s_}", [128, 4, 128], BF16), wTb=sb(f"wTb{s_}", [128, 4, 128], BF16),
                uf=sb(f"uf{s_}", [128, 4, 128]), qkT=sb(f"qkT{s_}", [128, 4, 128], BF16),
                qdT=sb(f"qdT{s_}", [128, 4, 128], BF16), vn=sb(f"vn{s_}", [128, 4, 128], BF16))
            d_["osq"] = d_["kbT"]
            d_["otmp"] = d_["kbg"]
            d_["rstd"] = d_["Rf"]
            d_["dsT"] = d_["uf"]
            DEL.append(d_)

        S.op("pool", lambda e: e.memset(Vwin[:], 0.0), writes=[("V", b) for b in range(1 + T // 128)])
        S.op("dve", lambda e: e.tensor_copy(out=dmr[:, 0, :], in_=dm[:, ONESF, :]), reads=["dm"], writes=["dmr"])
        S.op("dve", lambda e: e.tensor_copy(out=dmr[:, 1, :], in_=dm[:, IDF, :]), reads=["dm", "dmr"], writes=["dmr"])
        S.op("dve", lambda e: e.tensor_copy(out=mposr[:], in_=mpos4[:]), reads=["mpos4"], writes=["mposr"])

        for h in range(8):
            for part in range(3):
                cidx = part * 8 + h
                for j in range(4):
                    S.op("dve", lambda e: e.tensor_scalar(out=dgbuild[:, part * 4 + j, :], in0=ident_f[:], scalar1=convw[:, cidx, j:j + 1],
                                                          scalar2=None, op0=ALU.mult),
                         reads=["ident_f", "convw"], writes=[("dgb", 0)])
            S.dma("sp", dgscr[h], dgbuild[:], reads=[("dgb", 0)], writes=[("dgscr", h)])

        NWB = 3
        wbuf = [sb(f"wbuf{i}", [128, 8, 512], BF16) for i in range(NWB)]
        wstate = {"n": 0, "conv": set()}

        def wget(g):
            i = wstate["n"] % NWB
            wstate["n"] += 1
            wb = wbuf[i]; key = ("wbuf", i)
            nco = GCOLS[g]
            if g not in wstate["conv"]:
                wstate["conv"].add(g)
                for qd in range(4):
                    stg, sk = (yt, "yt") if qd % 2 == 0 else (ysq, "ysq")
                    sv = stg[:, :].rearrange("p (k c) -> p k c", k=2)
                    S.dma("sp", sv[:, :, 0:nco], wall[g, :, 2 * qd:2 * qd + 2, 0:nco], writes=[sk])
                    if qd % 2 == 0:
                        S.op("dve", lambda e: e.tensor_copy(out=wb[:, 2 * qd:2 * qd + 2, 0:nco], in_=sv[:, :, 0:nco]), reads=[sk], writes=[key])
                    else:
                        S.op("act", lambda e: e.copy(out=wb[:, 2 * qd:2 * qd + 2, 0:nco], in_=sv[:, :, 0:nco]), reads=[sk], writes=[key])
                S.dma("pool", wscr[g, :, :, 0:nco], wb[:, :, 0:nco], reads=[key], writes=[("wscr", g)])
            else:
                S.dma("sp", wb[:, :, 0:nco], wscr[g, :, :, 0:nco], reads=[("wscr", g)], writes=[key])
            return wb, key

        def v4(ap2d, h=4):
            return ap2d.rearrange("p (h n) -> p h n", h=h)

        def interleave(gens):
            gens = [g for g in gens if g is not None]
            while gens:
                alive = []
                for g in gens:
                    try:
                        next(g)
                        alive.append(g)
                    except StopIteration:
                        pass
                gens = alive

        def run_tile(kind, si, t0, Tn, first, last):
            bs = 128 if Tn >= 128 else Tn
            nblk = Tn // bs
            nch = bs // 64
            nk = 128 + bs
            xsrc = xp[si] if kind == "p" else xs
            ydst = yp[si] if kind == "p" else ys
            rp0 = t0 if kind == "p" else SEQ
            xkeys = [("xT", b) for b in range(nblk)]
            rr["pmlim"] = 2

            def proj(wb, wkey, c):
                pb, pk = next_pm()
                def f(e):
                    for k in range(8):
                        ins = e.matmul(out=pb[:, 0:Tn], lhsT=wb[:, k, c * 128:(c + 1) * 128], rhs=xT[:, k, 0:Tn],
                                       start=(k == 0), stop=(k == 7))
                    return ins
                S.op("pe", f, reads=[wkey] + xkeys, writes=[pk])
                return pb, pk

            def rope(pb, pk, out_ap, out_key, f32=None):
                S.op("act", lambda e: e.copy(out=tmpb[:, 0:Tn], in_=pb[:, 0:Tn]), reads=[pk], writes=["tmpb"])
                pwb, pwk = next_pw()
                S.op("pe", lambda e: e.matmul(out=pwb[:, 0:Tn], lhsT=pswap_b[:], rhs=tmpb[:, 0:Tn], start=True, stop=True),
                     reads=["tmpb", "pswap_b"], writes=[pwk])
                S.op("dve", lambda e: e.tensor_tensor(out=tmpf1[:, 0:Tn], in0=pwb[:, 0:Tn], in1=ropes[:, 0:Tn], op=ALU.mult),
                     reads=[pwk, "ropes"], writes=["tmpf1"])
                S.op("dve", lambda e: e.tensor_tensor(out=tmpf2[:, 0:Tn], in0=pb[:, 0:Tn], in1=ropec[:, 0:Tn], op=ALU.mult),
                     reads=[pk, "ropec"], writes=["tmpf2"])
                S.op("dve", lambda e: e.tensor_tensor(out=out_ap, in0=tmpf1[:, 0:Tn], in1=tmpf2[:, 0:Tn], op=ALU.add),
                     reads=["tmpf1", "tmpf2"], writes=[out_key])
                if f32 is not None:
                    fa, fk, nl = f32
                    S.op("pool", lambda e: e.tensor_tensor(out=fa, in0=tmpf1[:, Tn - nl:Tn], in1=tmpf2[:, Tn - nl:Tn], op=ALU.add),
                         reads=["tmpf1", "tmpf2"], writes=[fk])

            for b in range(nblk):
                r0 = t0 + b * bs
                S.dma("sp", xin[0:bs, :], xsrc[r0:r0 + bs, :], writes=["xin"])
                if b % 2 == 0:
                    S.op("dve", lambda e: e.tensor_copy(out=xbf[0:bs, :], in_=xin[0:bs, :]), reads=["xin"], writes=["xbf"])
                else:
                    S.op("act", lambda e: e.copy(out=xbf[0:bs, :], in_=xin[0:bs, :]), reads=["xin"], writes=["xbf"])
                def tr(e):
                    for k in range(8):
                        ins = e.transpose(out=ptb[:, k * 128:k * 128 + bs], in_=xbf[0:bs, k * 128:(k + 1) * 128],
                                          identity=ident_b[0:bs, 0:bs])
                    return ins
                S.op("pe", tr, reads=["xbf", "ident_b"], writes=["ptb"])
                S.op("act", lambda e: e.copy(out=xT[:, :, b * 128:b * 128 + bs], in_=v4(ptb[:, :], 8)[:, :, 0:bs]),
                     reads=["ptb"], writes=[("xT", b)])
            S.dma("sp", ropec[:, 0:Tn], ropec_d[:, rp0:rp0 + Tn], writes=["ropec"])
            S.dma("sp", ropes[:, 0:Tn], ropes_d[:, rp0:rp0 + Tn], writes=["ropes"])
            if DBG["stage"] < 1:
                return

            if first:
                if kind == "p":
                    S.op("pool", lambda e: e.memset(Kwin[:, :, 0:128], 0.0), writes=["Kprev"])
                    S.op("pool", lambda e: e.memset(Vwin[:, 0, :, :, :], 0.0), writes=[("V", 0)])
                    S.op("pool", lambda e: e.memset(hist[:], 0.0), writes=["hist"])
                    S.op("pool", lambda e: e.memset(Sf[:], 0.0), writes=[("Sf", h) for h in range(8)])
                    S.op("pool", lambda e: e.memset(Sb[:], 0.0), writes=[("Sb", 0), ("Sb", 1)])
                else:
                    S.dma("sp", xres[:, 0:512], ckd, writes=["xres"])
                    for g in range(4):
                        pwb, pwk = next_pw()
                        S.op("pe", lambda e: e.transpose(out=pwb[:, 0:128], in_=xres[:, g * 128:(g + 1) * 128], identity=ident_f[:]),
                             reads=["xres", "ident_f"], writes=[pwk])
                        S.op("act", lambda e: e.copy(out=Kwin[:, g, 0:128], in_=pwb[:, 0:128]), reads=[pwk], writes=["Kprev"])
                    S.dma("sp", vf[:], cv, writes=["vf"])
                    S.op("act", lambda e: e.copy(out=Vwin[:, 0, :, 0, 0:64], in_=v4(vf[:, :])), reads=["vf"], writes=[("V", 0)])
                    S.op("dve", lambda e: e.tensor_copy(out=Vwin[:, 0, :, 1, 64:128], in_=v4(vf[:, :])), reads=["vf", ("V", 0)], writes=[("V", 0)])
                    S.dma("sp", hist[:], sconv, writes=["hist"])
                    S.dma("sp", Sf[:], sdelta.rearrange("h d e -> d h e"), writes=[("Sf", h) for h in range(8)])
                    S.op("act", lambda e: e.copy(out=Sb[:], in_=Sf[:]), reads=[("Sf", h) for h in range(8)], writes=[("Sb", 0), ("Sb", 1)])
                    S.dma("pool", nks[0:64, :], ck[64:128, :])
                    S.dma("pool", nvs[0:64, :], cv[64:128, :])

            nl = min(128, Tn)
            wb, wk = wget(G_K)
            for g in range(4):
                pb, pk = proj(wb, wk, g)
                rope(pb, pk, Kwin[:, g, 128:128 + Tn], ("Kcur", g), f32=(kf[:, g, 0:nl], ("kf", g), nl) if last else None)
            if DBG["stage"] < 2:
                return
            wb, wk = wget(G_VAB)
            for b in range(nblk):
                pb_, pk = next_pw()
                def fv(e):
                    for k in range(8):
                        ins = e.matmul(out=pb_[0:bs, 0:272], lhsT=xT[:, k, b * 128:b * 128 + bs], rhs=wb[:, k, 0:272],
                                       start=(k == 0), stop=(k == 7))
                    return ins
                S.op("pe", fv, reads=[wk, ("xT", b)], writes=[pk])
                S.op("act", lambda e: e.copy(out=Vwin[0:bs, 1 + b, :, 0, 0:64], in_=v4(pb_[0:bs, 0:256])), reads=[pk], writes=[("V", 1 + b)])
                S.op("dve", lambda e: e.tensor_copy(out=Vwin[0:bs, 1 + b, :, 1, 64:128], in_=v4(pb_[0:bs, 0:256])),
                     reads=[pk, ("V", 1 + b)], writes=[("V", 1 + b)])
                S.op("dve", lambda e: e.tensor_copy(out=abt[0:bs, b, :], in_=pb_[0:bs, 256:272]), reads=[pk], writes=["abt"])
                if last and b == nblk - 1:
                    S.op("act", lambda e: e.copy(out=vf[0:bs, :], in_=pb_[0:bs, 0:256]), reads=[pk], writes=["vf"])
                    if kind == "p":
                        S.dma("pool", nvp[si], vf[:, :], reads=["vf"])
                    else:
                        S.dma("pool", nvs[64:128, :], vf[0:64, :], reads=["vf"])
            nb_ = nblk
            S.op("dve", lambda e: e.tensor_tensor(out=tmpab[0:bs, 0:nb_, :], in0=abt[0:bs, 0:nb_, 0:8],
                                                  in1=dtb[0:bs, :].unsqueeze(1).to_broadcast([bs, nb_, 8]), op=ALU.add),
                 reads=["abt", "dtb"], writes=["tmpab"])
            S.op("act", lambda e: e.activation(out=tmpab[0:bs, 0:nb_, :], in_=tmpab[0:bs, 0:nb_, :], func=AF.Exp), reads=["tmpab"], writes=["tmpab"])
            S.op("act", lambda e: e.activation(out=tmpab[0:bs, 0:nb_, :], in_=tmpab[0:bs, 0:nb_, :], func=AF.Ln, bias=1.0, scale=1.0),
                 reads=["tmpab"], writes=["tmpab"])
            S.op("dve", lambda e: e.tensor_tensor(out=gtok[0:bs, 0:nb_, :], in0=tmpab[0:bs, 0:nb_, :],
                                                  in1=nega[0:bs, :].unsqueeze(1).to_broadcast([bs, nb_, 8]), op=ALU.mult),
                 reads=["tmpab", "nega"], writes=["gtok"])
            S.op("act", lambda e: e.activation(out=btok[0:bs, 0:nb_, :], in_=abt[0:bs, 0:nb_, 8:16], func=AF.Sigmoid), reads=["abt"], writes=["btok"])
            if DBG["stage"] < 3:
                return
            for half, gq in enumerate((G_Q0, G_Q1)):
                wb, wk = wget(gq)
                for c in range(4):
                    i = half * 4 + c
                    pb, pk = proj(wb, wk, c)
                    rope(pb, pk, QT[:, i, 0:Tn], ("QT", i))
            rr["pmlim"] = 5
            for half, gz in enumerate((G_ZA0, G_ZA1)):
                wb, wk = wget(gz)
                for c in range(4):
                    i = half * 4 + c
                    pb, pk = proj(wb, wk, c)
                    S.op("act", lambda e: e.activation(out=bufZ[:, i, 0:Tn], in_=pb[:, 0:Tn], func=AF.Silu), reads=[pk], writes=[("bufZ", i)])
            if DBG["stage"] < 4:
                return

            def att_unit(s, b, g):
                A = ATT[s]
                firstblk = first and b == 0 and kind == "p"
                mcol = 128 + (256 if firstblk else 0)
                pa, pak = pw[2 * s], ("pw", 2 * s)
                pb2, pbk = pw[2 * s + 1], ("pw", 2 * s + 1)
                banks = [pa, pb2, pa, pb2]; bkeys = [pak, pbk, pak, pbk]
                for hh in range(4):
                    h = 4 * g + hh; i = h // 2; r0 = (h % 2) * 64
                    def fs(e):
                        o = banks[hh][:, (hh // 2) * 256:(hh // 2) * 256 + 256]
                        ins = e.matmul(out=o, lhsT=QT[r0:r0 + 64, i, b * 128:b * 128 + 128],
                                       rhs=Kwin[r0:r0 + 64, g, b * 128:b * 128 + 256], start=True, stop=(kind != "p"))
                        if kind == "p":
                            ins = e.matmul(out=o, lhsT=amask_b[0:3, 0:128], rhs=amask_b[0:3, mcol:mcol + 256], start=False, stop=True)
                        return ins
                    S.op("pe", fs, reads=[("QT", i), "amask_b", "Kprev", ("Kcur", g)], writes=[bkeys[hh]])
                yield
                for j, (bk, bkk) in enumerate(((pa, pak), (pb2, pbk))):
                    S.op("dve", lambda e: e.reduce_max(out=A["mx"][0:bs, j:4:2], in_=v4(bk[0:bs, :], 2)[:, :, 0:nk], axis=AX.X),
                         reads=[bkk], writes=[("mx", s, j)])
                S.op("dve", lambda e: e.scalar_tensor_tensor(out=A["ng"][0:bs, :], in0=A["mx"][0:bs, :], scalar=-0.125,
                                                             in1=nsinks[0:bs, 4 * g:4 * g + 4], op0=ALU.mult, op1=ALU.min),
                     reads=[("mx", s, 0), ("mx", s, 1), "nsinks"], writes=[("ng", s)])
                S.op("pool", lambda e: e.memset(A["sm"][:], 0.0), writes=[("sm", s)])
                yield
                for hh in range(4):
                    S.op("act", lambda e: e.activation(out=A["Pm"][0:bs, hh, 0:nk],
                                                       in_=banks[hh][0:bs, (hh // 2) * 256:(hh // 2) * 256 + nk],
                                                       func=AF.Exp, bias=A["ng"][0:bs, hh:hh + 1], scale=0.125,
                                                       accum_out=A["sm"][0:bs, hh:hh + 1]),
                         reads=[bkeys[hh], ("ng", s), ("sm", s)], writes=[("Pm", s, hh), ("sm", s)])
                S.op("dve", lambda e: e.tensor_tensor(out=A["es"][0:bs, :], in0=sinks[0:bs, 4 * g:4 * g + 4], in1=A["ng"][0:bs, :], op=ALU.add),
                     reads=["sinks", ("ng", s)], writes=[("es", s)])
                S.op("act", lambda e: e.activation(out=A["es"][0:bs, :], in_=A["es"][0:bs, :], func=AF.Exp), reads=[("es", s)], writes=[("es", s)])
                yield
                S.op("dve", lambda e: e.tensor_tensor(out=A["es"][0:bs, :], in0=A["es"][0:bs, :], in1=A["sm"][0:bs, :], op=ALU.add),
                     reads=[("es", s), ("sm", s)], writes=[("es", s)])
                S.op("dve", lambda e: e.reciprocal(out=A["rd"][0:bs, :], in_=A["es"][0:bs, :]), reads=[("es", s)], writes=[("rd", s)])
                S.op("dve", lambda e: e.tensor_tensor(out=A["Pn"][0:bs, :, 0:nk], in0=A["Pm"][0:bs, :, 0:nk],
                                                      in1=A["rd"][0:bs, :].unsqueeze(2).to_broadcast([bs, 4, nk]), op=ALU.mult),
                     reads=[("Pm", s, hh) for hh in range(4)] + [("rd", s)], writes=[("Pn", s)])
                yield
                def ftr(e):
                    for hh in range(4):
                        ins = e.transpose(out=ptb[:, (2 * hh) * 128:(2 * hh) * 128 + bs], in_=A["Pn"][0:bs, hh, 0:128], identity=ident_b[0:bs, 0:bs])
                        ins = e.transpose(out=ptb[0:bs, (2 * hh + 1) * 128:(2 * hh + 1) * 128 + bs], in_=A["Pn"][0:bs, hh, 128:128 + bs],
                                          identity=ident_b[0:bs, 0:bs])
                    return ins
                S.op("pe", ftr, reads=[("Pn", s), "ident_b"], writes=["ptb"])
                S.op("act", lambda e: e.copy(out=A["PT"][:, :, 0:bs], in_=v4(ptb[:, :], 8)[:, :, 0:bs]), reads=["ptb"], writes=[("PT", s)])
                yield
                po = (pO if s == 0 else pmb[1])[:, 0:256]
                pok = "pO" if s == 0 else ("pm", 1)
                def fpv(e):
                    for c2 in range(2):
                        for s2 in range(2):
                            hh = 2 * c2 + s2
                            o = po[:, c2 * 128:c2 * 128 + bs]
                            e.matmul(out=o, lhsT=Vwin[:, b, g, s2, :], rhs=A["PT"][:, 2 * hh, 0:bs], start=(s2 == 0), stop=False)
                            ins = e.matmul(out=o, lhsT=Vwin[0:bs, b + 1, g, s2, :], rhs=A["PT"][0:bs, 2 * hh + 1, 0:bs],
                                           start=False, stop=(s2 == 1))
                    return ins
                S.op("pe", fpv, reads=[("PT", s), ("V", b), ("V", b + 1)], writes=[pok])
                yield
                S.op("dve", lambda e: e.tensor_tensor(out=bufG[:, 2 * g:2 * g + 2, b * 128:b * 128 + bs], in0=v4(po, 2)[:, :, 0:bs],
                                                      in1=bufZ[:, 2 * g:2 * g + 2, b * 128:b * 128 + bs], op=ALU.mult),
                     reads=[pok, ("bufZ", 2 * g), ("bufZ", 2 * g + 1)], writes=[("bufG", 2 * g, b), ("bufG", 2 * g + 1, b)])
                yield

            def att_stream(s):
                units = [(b, g) for b in range(nblk) for g in range(4)]
                for (b, g) in units[s::2]:
                    yield from att_unit(s, b, g)

            def conv_head(h):
                wb, wk = wget(G_D0 + h)
                dg = dgb[h % 2]; dgk = ("dgb", h % 2)
                ub = ubs[h % 2]; ubn = "ub%d" % (h % 2)
                S.dma("sp", dg[:], dgscr[h], reads=[("dgscr", h)], writes=[dgk])
                if DBG.get("conv", 9) < 1:
                    return
                for part in range(3):
                    cidx = part * 8 + h
                    pb, pk = proj(wb, wk, part)
                    if DBG.get("conv", 9) == 1:
                        cx = DBG.get("convx", 0)
                        if cx == 1:
                            S.op("dve", lambda e: e.tensor_copy(out=hist[:, :, cidx], in_=pb[:, Tn - 3:Tn]), reads=[pk, "hist"], writes=["hist"])
                        if cx == 4:
                            S.op("act", lambda e: e.copy(out=hist[:, :, cidx], in_=pb[:, Tn - 3:Tn]), reads=[pk, "hist"], writes=["hist"])
                        if cx == 2:
                            S.op("dve", lambda e: e.tensor_copy(out=ub[:, part, 0:3], in_=hist[:, :, cidx]), reads=["hist"], writes=[(ubn, part)])
                        if cx == 3:
                            S.op("act", lambda e: e.copy(out=ub[:, part, 3:3 + Tn], in_=pb[:, 0:Tn]), reads=[pk, (ubn, part)], writes=[(ubn, part)])
                        else:
                            S.op("act", lambda e: e.copy(out=ub[:, part, 4:4 + Tn], in_=pb[:, 0:Tn]), reads=[pk, (ubn, part)], writes=[(ubn, part)])
                        yield
                        continue
                    S.op("dve", lambda e: e.tensor_copy(out=ub[:, part, 0:3], in_=hist[:, :, cidx]), reads=["hist"], writes=[(ubn, part)])
                    S.op("act", lambda e: e.copy(out=hist[:, :, cidx], in_=pb[:, Tn - 3:Tn]), reads=[pk, "hist"], writes=["hist"])
                    S.op("act", lambda e: e.copy(out=ub[:, part, 3:3 + Tn], in_=pb[:, 0:Tn]), reads=[pk, (ubn, part)], writes=[(ubn, part)])
                    yield
                if DBG.get("conv", 9) < 2:
                    return
                for part in range(3):
                    pc, pck = next_pm()
                    def fc(e):
                        for j in range(4):
                            ins = e.matmul(out=pc[:, 0:Tn], lhsT=dg[:, part * 4 + j, :], rhs=ub[:, part, j:j + Tn], start=(j == 0), stop=(j == 3))
                        return ins
                    S.op("pe", fc, reads=[dgk, (ubn, part)], writes=[pck])
                    dst, nm = ((qTa, "qTa"), (kTa, "kTa"), (vTa, "vTa"))[part]
                    S.op("act", lambda e: e.activation(out=dst[:, h, 0:Tn], in_=pc[:, 0:Tn], func=AF.Silu), reads=[pck], writes=[(nm, h)])
                    yield

            def filler_att():
                for half, gg in enumerate((G_GA0, G_GA1)):
                    wb, wk = wget(gg)
                    for c in range(4):
                        i = half * 4 + c
                        pb, pk = proj(wb, wk, c)
                        S.op("act", lambda e: e.activation(out=bufGate[:, i, 0:Tn], in_=pb[:, 0:Tn], func=AF.Sigmoid), reads=[pk], writes=[("bufGate", i)])
                        yield
                if DBG["stage"] >= 6:
                    for h in range(8):
                        yield from conv_head(h)

            rr["pmlim"] = 1
            interleave([att_stream(0), att_stream(1)])
            rr["pmlim"] = 5
            interleave([filler_att()])
            if not last:
                S.op("pool", lambda e: e.tensor_copy(out=Kwin[:, :, 0:128], in_=Kwin[:, :, Tn:Tn + 128]),
                     reads=[("Kcur", g) for g in range(4)] + ["Kprev"], writes=["Kprev"])
                S.op("pool", lambda e: e.tensor_copy(out=Vwin[:, 0, :, :, :], in_=Vwin[:, nblk, :, :, :]),
                     reads=[("V", nblk), ("V", 0)], writes=[("V", 0)])
            if DBG["stage"] < 5:
                return
            if last:
                for g in range(4):
                    pwb, pwk = next_pw()
                    S.op("pe", lambda e: e.transpose(out=pwb[0:nl, 0:128], in_=kf[:, g, 0:nl], identity=ident_f[:]),
                         reads=[("kf", g), "ident_f"], writes=[pwk])
                    S.op("dve", lambda e: e.tensor_copy(out=yt[0:nl, g * 64:(g + 1) * 64], in_=pwb[0:nl, 0:64]), reads=[pwk], writes=["yt"])
                if kind == "p":
                    S.dma("pool", nkp[si], yt[:, 0:256], reads=["yt"])
                else:
                    S.dma("pool", nks[64:128, :], yt[0:64, 0:256], reads=["yt"])
            gkeys = [("bufG", ec, b) for ec in range(8) for b in range(nblk)]
            for half, go in enumerate((G_OA0, G_OA1)):
                wb, wk = wget(go)
                for c in range(4):
                    dd = half * 4 + c
                    pb, pk = next_pm()
                    def fo(e):
                        for ec in range(8):
                            ins = e.matmul(out=pb[:, 0:Tn], lhsT=wb[:, ec, c * 128:(c + 1) * 128], rhs=bufG[:, ec, 0:Tn], start=(ec == 0), stop=(ec == 7))
                        return ins
                    S.op("pe", fo, reads=[wk] + gkeys, writes=[pk])
                    S.op("dve", lambda e: e.tensor_tensor(out=hA[:, dd, 0:Tn], in0=pb[:, 0:Tn], in1=bufGate[:, dd, 0:Tn], op=ALU.mult),
                         reads=[pk, ("bufGate", dd)], writes=[("hA", dd)])
            if DBG["stage"] < 6 or DBG.get("conv", 9) < 3:
                return
            if last:
                pwb, pwk = next_pw()
                S.op("pe", lambda e: e.transpose(out=pwb[0:72, 0:128], in_=hist[:, :, :].rearrange("p t c -> p (t c)"), identity=ident_f[:]),
                     reads=["hist", "ident_f"], writes=[pwk])
                S.op("dve", lambda e: e.tensor_copy(out=trf[0:72, :], in_=pwb[0:72, 0:128]), reads=[pwk], writes=["trf"])
                S.dma("pool", ncp[si] if kind == "p" else ncs, trf[0:72, :], reads=["trf"])
            for which, (dst, sqbuf) in enumerate(((qTa, bufG), (kTa, bufGate))):
                nm = "qTa" if which == 0 else "kTa"
                allk = [(nm, h) for h in range(8)]
                bk_ = gkeys if which == 0 else [("bufGate", dd) for dd in range(8)]
                S.op("act", lambda e: e.activation(out=sqbuf[:, :, 0:Tn], in_=dst[:, :, 0:Tn], func=AF.Square), reads=allk, writes=bk_)
                for hp in range(4):
                    pwb, pwk = next_pm_full()
                    S.op("pe", lambda e: e.matmul(out=v4(pwb[:, :], 2)[:, :, 0:Tn], lhsT=ones_b[:], rhs=sqbuf[:, 2 * hp:2 * hp + 2, 0:Tn], start=True, stop=True),
                         reads=bk_ + ["ones_b"], writes=[pwk])
                    S.op("act", lambda e: e.activation(out=rnf[:, :, 0:Tn], in_=v4(pwb[:, :], 2)[:, :, 0:Tn], func=AF.Ln, bias=1e-6, scale=1.0),
                         reads=[pwk], writes=["rnf"])
                    S.op("act", lambda e: e.activation(out=rnf[:, :, 0:Tn], in_=rnf[:, :, 0:Tn], func=AF.Exp, scale=-0.5), reads=["rnf"], writes=["rnf"])
                    hk = [(nm, 2 * hp), (nm, 2 * hp + 1)]
                    if which == 0:
                        S.op("dve", lambda e: e.scalar_tensor_tensor(out=dst[:, 2 * hp:2 * hp + 2, 0:Tn], in0=dst[:, 2 * hp:2 * hp + 2, 0:Tn], scalar=float(128.0 ** -0.5),
                                                                     in1=rnf[:, :, 0:Tn], op0=ALU.mult, op1=ALU.mult),
                             reads=["rnf"] + hk, writes=hk)
                    else:
                        S.op("dve", lambda e: e.tensor_tensor(out=dst[:, 2 * hp:2 * hp + 2, 0:Tn], in0=dst[:, 2 * hp:2 * hp + 2, 0:Tn], in1=rnf[:, :, 0:Tn], op=ALU.mult),
                             reads=["rnf"] + hk, writes=hk)
            if DBG["stage"] < 7:
                return
            for b in range(nblk):
                bsl = slice(b * 128, b * 128 + bs)
                for hg in range(2):
                    def ftk(e):
                        for hq in range(4):
                            h = 4 * hg + hq
                            e.transpose(out=ptb[0:bs, hq * 128:(hq + 1) * 128], in_=kTa[:, h, bsl], identity=ident_b[:])
                            ins = e.transpose(out=ptb[0:bs, 512 + hq * 128:512 + (hq + 1) * 128], in_=vTa[:, h, bsl], identity=ident_b[:])
                        return ins
                    hl_ = list(range(4 * hg, 4 * hg + 4))
                    S.op("pe", ftk, reads=[("kTa", h) for h in hl_] + [("vTa", h) for h in hl_] + ["ident_b"], writes=["ptb"])
                    S.op("dve", lambda e: e.tensor_copy(out=ktok[0:bs, b, 4 * hg:4 * hg + 4, :], in_=v4(ptb[0:bs, 0:512])), reads=["ptb"], writes=[("ktok", b, hg)])
                    S.op("dve", lambda e: e.tensor_copy(out=vtok[0:bs, b, 4 * hg:4 * hg + 4, :], in_=v4(ptb[0:bs, 512:1024])), reads=["ptb"], writes=[("vtok", b, hg)])
            for half, gz in enumerate((G_ZB0, G_ZB1)):
                wb, wk = wget(gz)
                for c in range(4):
                    i = half * 4 + c
                    pb, pk = proj(wb, wk, c)
                    S.op("act", lambda e: e.activation(out=bufZ[:, i, 0:Tn], in_=pb[:, 0:Tn], func=AF.Silu), reads=[pk], writes=[("bufZ", i)])
            if DBG["stage"] < 8:
                return

            def delta_unit(s, b, hg):
                Dd = DEL[s]
                blk = slice(b * 128, b * 128 + bs)
                hs = slice(4 * hg, 4 * hg + 4)
                hl = list(range(4 * hg, 4 * hg + 4))
                K = lambda n: (n, s)
                Ab, Nb, Rf, Rb = Dd["Ab"], Dd["Nb"], Dd["Rf"], Dd["Rb"]
                pod = pO if s == 0 else pmb[1]
                podk = ["pO"] if s == 0 else [("pm", 1)]
                S.op("dve", lambda e: e.tensor_tensor(out=Dd["kbT"][:, :, 0:bs], in0=kTa[:, hs, blk], in1=bbs[:, hs, 0:bs], op=ALU.mult),
                     reads=[("kTa", h) for h in hl] + [("bbs", hg)], writes=[K("kbT")])
                S.op("pool", lambda e: e.tensor_tensor(out=Dd["dsT"][0:bs, :, 0:bs], in0=dcT[0:bs, hs, 0:bs],
                                                       in1=dm[0:bs, OFFD, 0:bs].unsqueeze(1).to_broadcast([bs, 4, bs]), op=ALU.mult),
                     reads=[("dcT", h) for h in hl] + ["dm"], writes=[K("uf")])
                pa_, pak = next_ps(s)
                def fkk(e):
                    for hq, h in enumerate(hl):
                        ins = e.matmul(out=pa_[0:bs, hq * 128:hq * 128 + bs], lhsT=kTa[:, h, blk], rhs=Dd["kbT"][:, hq, 0:bs], start=True, stop=True)
                    return ins
                S.op("pe", fkk, reads=[("kTa", h) for h in hl] + [K("kbT")], writes=[pak])
                yield
                S.op("dve", lambda e: e.scalar_tensor_tensor(out=Ab[0][0:bs, :, 0:bs], in0=v4(pa_[0:bs, :])[:, :, 0:bs], scalar=-1.0,
                                                             in1=Dd["dsT"][0:bs, :, 0:bs], op0=ALU.mult, op1=ALU.mult),
                     reads=[pak, K("uf")], writes=[K("Ab0")])
                def ftA(e):
                    for hq in range(4):
                        ins = e.transpose(out=ptb[0:bs, hq * 128:hq * 128 + bs], in_=Ab[0][0:bs, hq, 0:bs], identity=ident_b[0:bs, 0:bs])
                    return ins
                S.op("pe", ftA, reads=[K("Ab0"), "ident_b"], writes=["ptb"])
                S.op("act", lambda e: e.copy(out=Nb[0][0:bs, :, 0:bs], in_=v4(ptb[0:bs, 0:512])[:, :, 0:bs]), reads=["ptb"], writes=[K("Nb0")])
                S.op("pool", lambda e: e.tensor_tensor(out=Rf[0:bs, :, 0:bs], in0=Ab[0][0:bs, :, 0:bs],
                                                       in1=dm[0:bs, IDF, 0:bs].unsqueeze(1).to_broadcast([bs, 4, bs]), op=ALU.add),
                     reads=[K("Ab0"), "dm"], writes=[K("Rf")])
                S.op("act", lambda e: e.copy(out=Rb[0:bs, :, 0:bs], in_=Rf[0:bs, :, 0:bs]), reads=[K("Rf")], writes=[K("Rb")])
                yield
                cur = 0
                for lev in range(5):
                    nxt = 1 - cur
                    ck_, nk_ = K("Ab%d" % cur), K("Nb%d" % cur)
                    pn_, pnk = next_ps(s)
                    def fN(e):
                        for hq in range(4):
                            ins = e.matmul(out=pn_[0:bs, hq * 128:hq * 128 + bs], lhsT=Ab[cur][0:bs, hq, 0:bs], rhs=Nb[cur][0:bs, hq, 0:bs], start=True, stop=True)
                        return ins
                    S.op("pe", fN, reads=[ck_, nk_], writes=[pnk])
                    if lev < 4:
                        pa2, pa2k = next_ps(s)
                        def fA(e):
                            for hq in range(4):
                                ins = e.matmul(out=pa2[0:bs, hq * 128:hq * 128 + bs], lhsT=Nb[cur][0:bs, hq, 0:bs], rhs=Ab[cur][0:bs, hq, 0:bs], start=True, stop=True)
                            return ins
                        S.op("pe", fA, reads=[ck_, nk_], writes=[pa2k])
                    yield
                    S.op("dve", lambda e: e.tensor_copy(out=Nb[nxt][0:bs, :, 0:bs], in_=v4(pn_[0:bs, :])[:, :, 0:bs]), reads=[pnk], writes=[K("Nb%d" % nxt)])
                    if lev < 4:
                        S.op("act", lambda e: e.copy(out=Ab[nxt][0:bs, :, 0:bs], in_=v4(pa2[0:bs, :])[:, :, 0:bs]), reads=[pa2k], writes=[K("Ab%d" % nxt)])
                    pr_, prk = next_ps(s)
                    def fR(e):
                        for hq in range(4):
                            ins = e.matmul(out=pr_[0:bs, hq * 128:hq * 128 + bs], lhsT=Nb[nxt][0:bs, hq, 0:bs], rhs=Rb[0:bs, hq, 0:bs], start=True, stop=True)
                        return ins
                    S.op("pe", fR, reads=[K("Nb%d" % nxt), K("Rb")], writes=[prk])
                    yield
                    S.op("dve", lambda e: e.tensor_tensor(out=Rf[0:bs, :, 0:bs], in0=Rf[0:bs, :, 0:bs], in1=v4(pr_[0:bs, :])[:, :, 0:bs], op=ALU.add),
                         reads=[prk, K("Rf")], writes=[K("Rf")])
                    S.op("act", lambda e: e.copy(out=Rb[0:bs, :, 0:bs], in_=Rf[0:bs, :, 0:bs]), reads=[K("Rf")], writes=[K("Rb")])
                    cur = nxt
                S.op("dve", lambda e: e.tensor_tensor(out=Dd["kbg"][0:bs, :, :], in0=ktok[0:bs, b, hs, :],
                                                      in1=sc1[0:bs, hs].unsqueeze(2).to_broadcast([bs, 4, 128]), op=ALU.mult),
                     reads=[("ktok", b, hg), "sc1"], writes=[K("kbg")])
                S.op("dve", lambda e: e.tensor_tensor(out=Dd["kdec"][0:bs, :, :], in0=ktok[0:bs, b, hs, :],
                                                      in1=edec[0:bs, hs].unsqueeze(2).to_broadcast([bs, 4, 128]), op=ALU.mult),
                     reads=[("ktok", b, hg), "edec"], writes=[K("kdec")])
                S.op("dve", lambda e: e.tensor_tensor(out=Dd["vbt"][0:bs, :, :], in0=vtok[0:bs, b, hs, :],
                                                      in1=btok[0:bs, b, hs].unsqueeze(2).to_broadcast([bs, 4, 128]), op=ALU.mult),
                     reads=[("vtok", b, hg), "btok"], writes=[K("vbt")])
                yield
                pw_, pwk_ = next_ps(s)
                def fw_(e):
                    for hq in range(4):
                        ins = e.matmul(out=pw_[:, hq * 128:hq * 128 + bs], lhsT=Dd["kbg"][0:bs, hq, :], rhs=Rb[0:bs, hq, 0:bs], start=True, stop=True)
                    return ins
                S.op("pe", fw_, reads=[K("kbg"), K("Rb")], writes=[pwk_])
                pu_, puk = next_ps(s)
                def fu(e):
                    for hq in range(4):
                        ins = e.matmul(out=pu_[0:bs, hq * 128:(hq + 1) * 128], lhsT=Rb[0:bs, hq, 0:bs], rhs=Dd["vbt"][0:bs, hq, :], start=True, stop=True)
                    return ins
                S.op("pe", fu, reads=[K("vbt"), K("Rb")], writes=[puk])
                yield
                S.op("act", lambda e: e.copy(out=Dd["wTb"][:, :, 0:bs], in_=v4(pw_[:, :])[:, :, 0:bs]), reads=[pwk_], writes=[K("wTb")])
                S.op("dve", lambda e: e.tensor_copy(out=Dd["uf"][0:bs, :, :], in_=v4(pu_[0:bs, :])), reads=[puk], writes=[K("uf")])
                pq_, pqk = next_ps(s)
                def fqk(e):
                    for hq, h in enumerate(hl):
                        ins = e.matmul(out=pq_[0:bs, hq * 128:hq * 128 + bs], lhsT=kTa[:, h, blk], rhs=qTa[:, h, blk], start=True, stop=True)
                    return ins
                S.op("pe", fqk, reads=[("kTa", h) for h in hl] + [("qTa", h) for h in hl], writes=[pqk])
                S.op("pool", lambda e: e.tensor_tensor(out=Dd["qdT"][:, :, 0:bs], in0=qTa[:, hs, blk], in1=E1[:, hs, 0:bs], op=ALU.mult),
                     reads=[("qTa", h) for h in hl] + [("E1", hg)], writes=[K("qdT")])
                yield
                S.op("dve", lambda e: e.tensor_tensor(out=Dd["qkT"][0:bs, :, 0:bs], in0=v4(pq_[0:bs, :])[:, :, 0:bs], in1=dcT[0:bs, hs, 0:bs], op=ALU.mult),
                     reads=[pqk] + [("dcT", h) for h in hl], writes=[K("qkT")])
                for c in range(nch):
                    r = slice(c * 64, (c + 1) * 64)
                    ps_, psk = next_ps(s)
                    def fws(e):
                        for hq, h in enumerate(hl):
                            ins = e.matmul(out=ps_[0:bs, hq * 128:(hq + 1) * 128], lhsT=Dd["wTb"][:, hq, 0:bs], rhs=Sb[:, h, :], start=True, stop=True)
                        return ins
                    S.op("pe", fws, reads=[K("wTb"), ("Sb", hg)], writes=[psk])
                    yield
                    S.op("dve", lambda e: e.tensor_tensor(out=Dd["vn"][r, :, :], in0=Dd["uf"][r, :, :], in1=v4(ps_[r, :]), op=ALU.subtract),
                         reads=[psk, K("uf")], writes=[K("vn")])
                    def fo_(e):
                        for hq, h in enumerate(hl):
                            o = pod[:, hq * 128 + c * 64:hq * 128 + (c + 1) * 64]
                            e.matmul(out=o, lhsT=Sb[:, h, :], rhs=Dd["qdT"][:, hq, r], start=True, stop=False)
                            ins = e.matmul(out=o, lhsT=Dd["vn"][r, hq, :], rhs=Dd["qkT"][r, hq, r], start=False, stop=True)
                        return ins
                    S.op("pe", fo_, reads=[("Sb", hg), K("qdT"), K("vn"), K("qkT")], writes=podk)
                    pS_, pSk = next_ps(s)
                    def fsu(e):
                        for hq in range(4):
                            ins = e.matmul(out=pS_[:, hq * 128:(hq + 1) * 128], lhsT=Dd["kdec"][r, hq, :], rhs=Dd["vn"][r, hq, :], start=True, stop=True)
                        return ins
                    S.op("pe", fsu, reads=[K("kdec"), K("vn")], writes=[pSk])
                    yield
                    for hq, h in enumerate(hl):
                        S.op("dve", lambda e: e.scalar_tensor_tensor(out=Sf[:, h, :], in0=Sf[:, h, :], scalar=ela[:, 8 * c + h:8 * c + h + 1],
                                                                     in1=pS_[:, hq * 128:(hq + 1) * 128], op0=ALU.mult, op1=ALU.add),
                             reads=[("Sf", h), "ela", pSk], writes=[("Sf", h)])
                    S.op("act", lambda e: e.copy(out=Sb[:, hs, :], in_=Sf[:, hs, :]), reads=[("Sf", h) for h in hl], writes=[("Sb", hg)])
                    yield
                S.op("act", lambda e: e.activation(out=Dd["osq"][:, :, 0:bs], in_=v4(pod[:, :])[:, :, 0:bs], func=AF.Square), reads=podk, writes=[K("kbT")])
                pss, pssk = next_ps(s)
                S.op("pe", lambda e: e.matmul(out=v4(pss[:, :])[:, :, 0:bs], lhsT=ones_b[:], rhs=Dd["osq"][:, :, 0:bs], start=True, stop=True),
                     reads=[K("kbT"), "ones_b"], writes=[pssk])
                yield
                S.op("act", lambda e: e.activation(out=Dd["rstd"][:, :, 0:bs], in_=v4(pss[:, :])[:, :, 0:bs], func=AF.Ln, bias=128.0 * 1e-6, scale=1.0),
                     reads=[pssk], writes=[K("Rf")])
                S.op("act", lambda e: e.activation(out=Dd["rstd"][:, :, 0:bs], in_=Dd["rstd"][:, :, 0:bs], func=AF.Exp, scale=-0.5),
                     reads=[K("Rf")], writes=[K("Rf")])
                S.op("dve", lambda e: e.tensor_tensor(out=Dd["otmp"][:, :, 0:bs], in0=v4(pod[:, :])[:, :, 0:bs], in1=Dd["rstd"][:, :, 0:bs], op=ALU.mult),
                     reads=podk + [K("Rf")], writes=[K("kbg")])
                S.op("dve", lambda e: e.scalar_tensor_tensor(out=bufG[:, hs, blk], in0=Dd["otmp"][:, :, 0:bs], scalar=nw2[:, 0:1],
                                                             in1=bufZ[:, hs, blk], op0=ALU.mult, op1=ALU.mult),
                     reads=[K("kbg"), "nw2"] + [("bufZ", h) for h in hl], writes=[("bufG", h, b) for h in hl])
                yield

            def gate_prep(b):
                g2 = gtok[0:bs, b, :]; b2 = btok[0:bs, b, :]
                pwb, pwk = next_pw()
                def fg(e):
                    e.matmul(out=pwb[0:bs, 0:8], lhsT=dm[0:bs, TRI, 0:bs], rhs=g2, start=True, stop=True)
                    e.matmul(out=pwb[0:bs, 8:16], lhsT=dm[0:bs, BO, 0:bs], rhs=g2, start=True, stop=True)
                    e.matmul(out=pwb[0:128, 16:24], lhsT=dm[0:bs, INDA, :], rhs=g2, start=True, stop=True)
                    return e.matmul(out=pwb[0:128, 24:32], lhsT=dm[0:bs, INDB, :], rhs=g2, start=True, stop=True)
                S.op("pe", fg, reads=["dm", "gtok"], writes=[pwk])
                S.op("dve", lambda e: e.tensor_copy(out=gsm[:, :], in_=pwb[:, 0:32]), reads=[pwk], writes=["gsm"])
                S.op("act", lambda e: e.activation(out=egc[0:bs, :], in_=gsm[0:bs, 0:8], func=AF.Exp), reads=["gsm"], writes=["egc"])
                S.op("dve", lambda e: e.tensor_scalar(out=ngc[0:bs, :], in0=gsm[0:bs, 0:8], scalar1=-1.0, scalar2=None, op0=ALU.mult), reads=["gsm"], writes=["ngc"])
                S.op("dve", lambda e: e.tensor_tensor(out=edec[0:bs, :], in0=gsm[0:bs, 8:16], in1=gsm[0:bs, 0:8], op=ALU.subtract), reads=["gsm"], writes=["edec"])
                S.op("act", lambda e: e.activation(out=edec[0:bs, :], in_=edec[0:bs, :], func=AF.Exp), reads=["edec"], writes=["edec"])
                S.op("act", lambda e: e.activation(out=ela[:, :], in_=gsm[:, 16:32], func=AF.Exp), reads=["gsm"], writes=["ela"])
                S.op("dve", lambda e: e.tensor_tensor(out=sc1[0:bs, :], in0=egc[0:bs, :], in1=b2, op=ALU.mult), reads=["egc", "btok"], writes=["sc1"])
                S.op("dve", lambda e: e.scalar_tensor_tensor(out=rhsb[0:bs, :, 0:bs], in0=g2.unsqueeze(2).to_broadcast([bs, 8, bs]), scalar=-1.0,
                                                             in1=dm[0:bs, TRI, 0:bs].unsqueeze(1).to_broadcast([bs, 8, bs]), op0=ALU.mult, op1=ALU.mult),
                     reads=["gtok", "dm"], writes=["rhsb"])
                for hf in range(2):
                    p1, p1k = next_pw()
                    S.op("pe", lambda e: e.matmul(out=v4(p1[:, :])[:, :, 0:bs], lhsT=dmr[0:bs, 0, :], rhs=rhsb[0:bs, 4 * hf:4 * hf + 4, 0:bs], start=True, stop=True),
                         reads=["dmr", "rhsb"], writes=[p1k])
                    S.op("act", lambda e: e.activation(out=E1[:, 4 * hf:4 * hf + 4, 0:bs], in_=v4(p1[:, :])[:, :, 0:bs], func=AF.Exp, scale=-1.0),
                         reads=[p1k], writes=[("E1", hf)])
                    p2, p2k = next_pw()
                    def fx(e):
                        o = v4(p2[0:bs, :])[:, :, 0:bs]
                        e.matmul(out=o, lhsT=dmr[0:bs, 0, 0:bs], rhs=rhsb[0:bs, 4 * hf:4 * hf + 4, 0:bs], start=True, stop=False)
                        return e.matmul(out=o, lhsT=dmr[0:bs, 1, 0:bs], rhs=v4(mposr[0:bs, :])[:, :, 0:bs], start=False, stop=True)
                    S.op("pe", fx, reads=["dmr", "rhsb", "mposr"], writes=[p2k])
                    for hq in range(4):
                        h = 4 * hf + hq
                        S.op("act", lambda e: e.activation(out=dcT[0:bs, h, 0:bs], in_=p2[0:bs, hq * 128:hq * 128 + bs], func=AF.Exp,
                                                           bias=ngc[0:bs, h:h + 1], scale=-1.0),
                             reads=[p2k, "ngc"], writes=[("dcT", h)])
                S.op("dve", lambda e: e.tensor_tensor(out=rhsb[0:bs, :, 0:bs], in0=b2.unsqueeze(2).to_broadcast([bs, 8, bs]),
                                                      in1=dm[0:bs, IDF, 0:bs].unsqueeze(1).to_broadcast([bs, 8, bs]), op=ALU.mult),
                     reads=["btok", "dm"], writes=["rhsb"])
                for hf in range(2):
                    p3, p3k = next_pw()
                    S.op("pe", lambda e: e.matmul(out=v4(p3[:, :])[:, :, 0:bs], lhsT=dmr[0:bs, 0, :], rhs=rhsb[0:bs, 4 * hf:4 * hf + 4, 0:bs], start=True, stop=True),
                         reads=["dmr", "rhsb"], writes=[p3k])
                    S.op("act", lambda e: e.copy(out=bbs[:, 4 * hf:4 * hf + 4, 0:bs], in_=v4(p3[:, :])[:, :, 0:bs]), reads=[p3k], writes=[("bbs", hf)])

            def filler_delta():
                for half, gg in enumerate((G_GB0, G_GB1)):
                    wb, wk = wget(gg)
                    for c in range(4):
                        i = half * 4 + c
                        pb, pk = proj(wb, wk, c)
                        S.op("act", lambda e: e.activation(out=bufGate[:, i, 0:Tn], in_=pb[:, 0:Tn], func=AF.Sigmoid), reads=[pk], writes=[("bufGate", i)])
                        yield

            interleave([filler_delta()])
            rr["pmlim"] = 1
            for b in range(nblk):
                gate_prep(b)
                interleave([delta_unit(0, b, 0), delta_unit(1, b, 1)])
            rr["pmlim"] = 5
            if last:
                S.dma("pool", (ndp[si] if kind == "p" else nds).rearrange("h d e -> d h e"), Sf[:, :, :], reads=[("Sf", h) for h in range(8)])
            if DBG["stage"] < 9:
                return

            for half, go in enumerate((G_OB0, G_OB1)):
                wb, wk = wget(go)
                for c in range(4):
                    dd = half * 4 + c
                    pb, pk = next_pm()
                    def fo2(e):
                        for ec in range(8):
                            ins = e.matmul(out=pb[:, 0:Tn], lhsT=wb[:, ec, c * 128:(c + 1) * 128], rhs=bufG[:, ec, 0:Tn], start=(ec == 0), stop=(ec == 7))
                        return ins
                    S.op("pe", fo2, reads=[wk] + gkeys, writes=[pk])
                    S.op("dve", lambda e: e.tensor_tensor(out=tmpf1[:, 0:Tn], in0=pb[:, 0:Tn], in1=bufGate[:, dd, 0:Tn], op=ALU.mult),
                         reads=[pk, ("bufGate", dd)], writes=["tmpf1"])
                    S.op("dve", lambda e: e.tensor_tensor(out=hA[:, dd, 0:Tn], in0=tmpf1[:, 0:Tn], in1=hA[:, dd, 0:Tn], op=ALU.add),
                         reads=["tmpf1", ("hA", dd)], writes=[("hA", dd)])
            if DBG["stage"] < 10:
                return
            wo0, wk0 = wget(G_OUT0)
            wo1, wk1 = wget(G_OUT1)
            hkeys = [("hA", dd) for dd in range(8)]
            for b in range(nblk):
                r0 = t0 + b * bs
                S.dma("sp", xres[0:bs, :], xsrc[r0:r0 + bs, :], writes=["xres"])
                S.op("pool", lambda e: e.memset(lnst[:], 0.0), writes=["lnst"])
                for half, (wo, wkk) in enumerate(((wo0, wk0), (wo1, wk1))):
                    pf, pfk = next_pm_full()
                    def ff(e):
                        for ec in range(8):
                            ins = e.matmul(out=pf[0:bs, :], lhsT=hA[:, ec, b * 128:b * 128 + bs], rhs=wo[:, ec, :], start=(ec == 0), stop=(ec == 7))
                        return ins
                    S.op("pe", ff, reads=[wkk] + hkeys, writes=[pfk])
                    S.op("dve", lambda e: e.scalar_tensor_tensor(out=yt[0:bs, half * 512:(half + 1) * 512], in0=xres[0:bs, half * 512:(half + 1) * 512],
                                                                 scalar=float(ALPHA), in1=pf[0:bs, :], op0=ALU.mult, op1=ALU.add),
                         reads=[pfk, "xres", "yt"], writes=["yt"])
                S.op("act", lambda e: e.activation(out=ysq[0:bs, :], in_=yt[0:bs, :], func=AF.Identity, accum_out=lnst[0:bs, 0:1]),
                     reads=["yt", "lnst"], writes=["ysq", "lnst"])
                S.op("act", lambda e: e.activation(out=ysq[0:bs, :], in_=yt[0:bs, :], func=AF.Square, accum_out=lnst[0:bs, 1:2]),
                     reads=["yt", "lnst", "ysq"], writes=["ysq", "lnst"])
                S.op("dve", lambda e: e.tensor_scalar(out=lnst[0:bs, 2:4], in0=lnst[0:bs, 0:2], scalar1=1.0 / D, scalar2=None, op0=ALU.mult),
                     reads=["lnst"], writes=["lnst"])
                S.op("dve", lambda e: e.tensor_tensor(out=lnst[0:bs, 4:5], in0=lnst[0:bs, 2:3], in1=lnst[0:bs, 2:3], op=ALU.mult), reads=["lnst"], writes=["lnst"])
                S.op("dve", lambda e: e.tensor_tensor(out=lnst[0:bs, 5:6], in0=lnst[0:bs, 3:4], in1=lnst[0:bs, 4:5], op=ALU.subtract), reads=["lnst"], writes=["lnst"])
                S.op("act", lambda e: e.activation(out=lnst[0:bs, 5:6], in_=lnst[0:bs, 5:6], func=AF.Ln, bias=1e-5, scale=1.0), reads=["lnst"], writes=["lnst"])
                S.op("act", lambda e: e.activation(out=lnst[0:bs, 6:7], in_=lnst[0:bs, 5:6], func=AF.Exp, scale=-0.5), reads=["lnst"], writes=["lnst"])
                S.op("dve", lambda e: e.tensor_scalar(out=ysq[0:bs, :], in0=yt[0:bs, :], scalar1=lnst[0:bs, 2:3], scalar2=lnst[0:bs, 6:7],
                                                      op0=ALU.subtract, op1=ALU.mult),
                     reads=["yt", "lnst", "ysq"], writes=["ysq"])
                S.op("dve", lambda e: e.tensor_tensor(out=ysq[0:bs, :], in0=ysq[0:bs, :], in1=lng[0:bs, :], op=ALU.mult), reads=["ysq", "lng"], writes=["ysq"])
                S.op("pool", lambda e: e.tensor_tensor(out=ysq[0:bs, :], in0=ysq[0:bs, :], in1=lnb[0:bs, :], op=ALU.add), reads=["ysq", "lnb"], writes=["ysq"])
                S.dma("pool", ydst[r0:r0 + bs, :], ysq[0:bs, :], reads=["ysq"])

        for si in range(0 if DBG.get("skip_prompt") else nps):
            nt = SEQ // T
            for ti in range(nt):
                run_tile("p", si, ti * T, T, ti == 0, ti == nt - 1)
        if with_sample:
            run_tile("s", 0, 0, DEC, True, True)
        print("[sched counts]", S.cnt, S.dma_cnt, flush=True)
        S.emit()
    return nc


_PROG = {}


def kernel(x_prompt, x_sample, cache_attn_k, cache_attn_v, state_conv, state_delta, w_in, attn_sinks, conv_w,
           a_log, dt_bias, delta_norm_w, w_o_attn, w_o_delta, w_out, ln_g, ln_b):
    f = lambda a: np.ascontiguousarray(np.asarray(a, dtype=np.float32))
    x_prompt = f(x_prompt); x_sample = f(x_sample)
    n = 8
    consts = _host_consts()
    wall = _wall(f(w_in), f(w_o_attn), f(w_o_delta), f(w_out))
    cw = f(conv_w)
    convw = np.ascontiguousarray(cw.reshape(4, 24, 128).transpose(2, 1, 0))
    common = dict(wall=wall, sinks=f(attn_sinks), convw=convw, alog=f(a_log), dtb=f(dt_bias),
                  normw=f(delta_norm_w).reshape(128, 1), lng=f(ln_g), lnb=f(ln_b), **consts)
    ck = f(cache_attn_k); cv = f(cache_attn_v); sc = f(state_conv); sd = f(state_delta)
    in_maps = []
    for c in range(n):
        ckc = ck[c].reshape(128, 4, 64)
        ckd = np.ascontiguousarray(np.stack([ckc, ckc], 2).reshape(128, 512))
        scv = np.ascontiguousarray(sc[c].reshape(3, 24, 128).transpose(2, 0, 1))
        in_maps.append(dict(xp=x_prompt[NPS * c:NPS * (c + 1)], xs=x_sample[c], ckd=ckd, ck=ck[c].reshape(128, 256),
                            cv=cv[c].reshape(128, 256), sconv=scv, sdelta=sd[c], **common))
    if "nc" not in _PROG:
        _PROG["nc"] = build_program()
    res = run_bass_kernel_spmd(_PROG["nc"], in_maps, core_ids=list(range(n)))
    R = res.results
    cat = lambda k: np.concatenate([R[c][k] for c in range(n)], 0)
    stk = lambda k: np.stack([R[c][k] for c in range(n)], 0)
    y_p = cat("yp")
    y_s = stk("ys")
    nk_p = cat("nkp").reshape(32, 128, 4, 64); nv_p = cat("nvp").reshape(32, 128, 4, 64)
    nc_p = cat("ncp").reshape(32, 3, 3072); nd_p = cat("ndp")
    nk_s = stk("nks").reshape(8, 128, 4, 64); nv_s = stk("nvs").reshape(8, 128, 4, 64)
    nc_s = stk("ncs").reshape(8, 3, 3072); nd_s = stk("nds")
    return (y_p, y_s, nk_p, nv_p, nc_p, nd_p, nk_s, nv_s, nc_s, nd_s)
```

```python
import numpy as np
from contextlib import ExitStack
import concourse.bass as bass
import concourse.mybir as mybir
from concourse.bass_utils import run_bass_kernel_spmd

F32 = mybir.dt.float32
BF16 = mybir.dt.bfloat16
F32R = mybir.dt.float32r
AF = mybir.ActivationFunctionType
ALU = mybir.AluOpType
AX = mybir.AxisListType

EPOCH = 30000
NDMA = 24

D = 1024
SEQ = 2048
NPS = 4
DEC = 64
PAST = 1024
T = 256
NG = 26
ALPHA = 2.0 ** 0.25
NEG = -30000.0
(G_K, G_VAB, G_Q0, G_Q1, G_ZA0, G_ZA1, G_OA0, G_OA1, G_GA0, G_GA1) = range(10)
G_D0 = 10
(G_ZB0, G_ZB1, G_GB0, G_GB1, G_OB0, G_OB1, G_OUT0, G_OUT1) = range(18, 26)
GCOLS = [512, 272, 512, 512, 512, 512, 512, 512, 512, 512] + [384] * 8 + [512] * 8


class Sched:
    ENG = ("pe", "act", "dve", "pool", "sp")

    def __init__(self, nc, stack):
        self.nc = nc
        self.stack = stack
        self.eng = {"pe": nc.tensor, "act": nc.scalar, "dve": nc.vector, "pool": nc.gpsimd, "sp": nc.sync}
        self.cnt = {e: 0 for e in self.ENG}
        self.sems = {e: [] for e in self.ENG}
        self.seen = {e: {} for e in self.ENG}
        self.lastw = {}
        self.readers = {}
        self.dma_sems = [stack.enter_context(nc.semaphore(f"dq{i}")) for i in range(NDMA)]
        self.dma_cnt = [0] * NDMA
        self.dma_rr = 0

    def _sem(self, e, n):
        idx = (n - 1) // EPOCH
        while len(self.sems[e]) <= idx:
            self.sems[e].append(self.stack.enter_context(
                self.nc.semaphore(f"s_{e}_{len(self.sems[e])}")))
        return self.sems[e][idx], (n - 1) % EPOCH + 1

    def _wait(self, e, ev):
        f, n = ev
        if self.seen[e].get(f, 0) >= n:
            return
        self.seen[e][f] = n
        if isinstance(f, tuple):
            self.eng[e].wait_ge(self.dma_sems[f[1]], 16 * n)
        else:
            sem, v = self._sem(f, n)
            self.eng[e].wait_ge(sem, v)

    def _deps(self, e, reads, writes):
        deps = set()
        for k in reads:
            w = self.lastw.get(k)
            if w is not None:
                deps.add(w)
        for k in writes:
            w = self.lastw.get(k)
            if w is not None:
                deps.add(w)
            for r in self.readers.get(k, ()):
                deps.add(r)
        for ev in sorted(deps, key=lambda x: (str(x[0]), x[1])):
            if ev[0] == e and e == "pe":
                continue
            self._wait(e, ev)

    def _commit(self, ev, reads, writes):
        for k in writes:
            self.lastw[k] = ev
            self.readers[k] = []
        for k in reads:
            self.readers.setdefault(k, []).append(ev)

    def op(self, e, fn, reads=(), writes=()):
        self._deps(e, reads, writes)
        self.cnt[e] += 1
        n = self.cnt[e]
        sem, _ = self._sem(e, n)
        fn(self.eng[e]).then_inc(sem, 1)
        ev = (e, n)
        self._commit(ev, reads, writes)
        return ev

    def dma(self, e, out, in_, reads=(), writes=(), **kw):
        self._deps(e, reads, writes)
        s = self.dma_rr
        self.dma_rr = (self.dma_rr + 1) % NDMA
        f = ("dma", s)
        if self.dma_cnt[s] > 0:
            self._wait(e, (f, self.dma_cnt[s]))
        self.dma_cnt[s] += 1
        n = self.dma_cnt[s]
        self.eng[e].dma_start(out=out, in_=in_, **kw).then_inc(self.dma_sems[s], 16)
        ev = (f, n)
        self._commit(ev, reads, writes)
        return ev

    def emit(self):
        for s in range(NDMA):
            if self.dma_cnt[s] > 0:
                self._wait("sp", (("dma", s), self.dma_cnt[s]))
        for e in self.ENG:
            if e != "sp" and self.cnt[e] > 0:
                self._wait("sp", (e, self.cnt[e]))


def _host_consts():
    c = {}
    c["ident"] = np.eye(128, dtype=np.float32)
    m = np.arange(128)
    d = m % 64
    pos = np.concatenate([np.arange(SEQ), PAST + np.arange(DEC)]).astype(np.float64)
    inv = 500000.0 ** (-np.arange(8) / 8.0)
    ang = pos[None, :] * inv[d % 8][:, None]
    rc = np.where((d < 16)[:, None], np.cos(ang), 1.0)
    rs = np.where((d < 8)[:, None], -np.sin(ang), np.where((d < 16)[:, None], np.sin(ang), 0.0))
    c["ropec"] = rc.astype(np.float32)
    c["ropes"] = rs.astype(np.float32)
    ps = np.zeros((128, 128), np.float32)
    for mm in range(128):
        if mm % 64 < 8:
            ps[mm + 8, mm] = 1.0
        elif mm % 64 < 16:
            ps[mm - 8, mm] = 1.0
    c["pswap"] = ps
    ml = np.zeros((3, 128), np.float32)
    ml[0, :64] = 1; ml[1, 64:] = 1; ml[2, :] = 1
    mg = np.zeros((3, 256), np.float32)
    mg[0, 192:] = NEG; mg[1, :64] = NEG
    mf = mg.copy(); mf[2, :128] = NEG
    c["amask"] = np.concatenate([ml, mg, mf], axis=1)
    same = (m[:, None] // 64) == (m[None, :] // 64)
    tri = (same & (m[:, None] <= m[None, :])).astype(np.float32)
    bo = same.astype(np.float32)
    inda = np.repeat((m < 64).astype(np.float32)[:, None], 128, 1)
    indb = np.repeat((m >= 64).astype(np.float32)[:, None], 128, 1)
    ones = np.ones((128, 128), np.float32)
    mpos = np.where(same & (m[None, :] >= m[:, None]), 0.0, 1e4).astype(np.float32)
    offd = (1.0 - np.eye(128)).astype(np.float32)
    c["dmats"] = np.stack([tri, bo, inda, indb, ones, offd, np.eye(128, dtype=np.float32)], 1)
    c["mpos4"] = np.tile(mpos, (1, 4)).astype(np.float32)
    return c


def _wall(w_in, w_o_attn, w_o_delta, w_out):
    cols = []
    kA = w_in[:, 1024:1280]; vA = w_in[:, 1280:1536]
    kd = np.concatenate([np.concatenate([kA[:, g * 64:(g + 1) * 64]] * 2, 1) for g in range(4)], 1)
    cols.append(kd)
    cols.append(np.concatenate([vA, w_in[:, 5632:5648]], 1))
    cols.append(w_in[:, 0:512]); cols.append(w_in[:, 512:1024])
    cols.append(w_in[:, 1536:2048]); cols.append(w_in[:, 2048:2560])
    cols.append(w_o_attn[:, 0:512]); cols.append(w_o_attn[:, 512:1024])
    cols.append(w_in[:, 6672:7184]); cols.append(w_in[:, 7184:7696])
    for h in range(8):
        cols.append(np.concatenate([w_in[:, 2560 + h * 128:2560 + (h + 1) * 128],
                                    w_in[:, 3584 + h * 128:3584 + (h + 1) * 128],
                                    w_in[:, 4608 + h * 128:4608 + (h + 1) * 128]], 1))
    cols.append(w_in[:, 5648:6160]); cols.append(w_in[:, 6160:6672])
    cols.append(w_in[:, 7696:8208]); cols.append(w_in[:, 8208:8720])
    cols.append(w_o_delta[:, 0:512]); cols.append(w_o_delta[:, 512:1024])
    cols.append(w_out[:, 0:512]); cols.append(w_out[:, 512:1024])
    wall = np.zeros((NG, 128, 8, 512), np.float32)
    for g, cm in enumerate(cols):
        assert cm.shape[1] == GCOLS[g]
        wall[g, :, :, :cm.shape[1]] = cm.reshape(8, 128, cm.shape[1]).transpose(1, 0, 2)
    return wall


DBG = {"stage": 99}
def build_program(nps=NPS, with_sample=True):
    nc = bass.Bass("TRN2", target_bir_lowering=False)

    def din(name, shape):
        return nc.dram_tensor(name, list(shape), F32, kind="ExternalInput").ap()

    def dout(name, shape):
        return nc.dram_tensor(name, list(shape), F32, kind="ExternalOutput").ap()

    xp = din("xp", [nps, SEQ, D]); xs = din("xs", [DEC, D])
    ckd = din("ckd", [128, 512]); ck = din("ck", [128, 256]); cv = din("cv", [128, 256])
    sconv = din("sconv", [128, 3, 24]); sdelta = din("sdelta", [8, 128, 128])
    wall = din("wall", [NG, 128, 8, 512])
    sinks_d = din("sinks", [16]); convw_d = din("convw", [128, 24, 4])
    alog_d = din("alog", [8]); dtb_d = din("dtb", [8]); normw_d = din("normw", [128, 1])
    lng_d = din("lng", [D]); lnb_d = din("lnb", [D])
    ident_d = din("ident", [128, 128]); ropec_d = din("ropec", [128, SEQ + DEC]); ropes_d = din("ropes", [128, SEQ + DEC])
    pswap_d = din("pswap", [128, 128]); amask_d = din("amask", [3, 640])
    dmats_d = din("dmats", [128, 7, 128]); mpos4_d = din("mpos4", [128, 512])

    yp = dout("yp", [nps, SEQ, D]); ys = dout("ys", [DEC, D])
    nkp = dout("nkp", [nps, 128, 256]); nvp = dout("nvp", [nps, 128, 256])
    ncp = dout("ncp", [nps, 72, 128]); ndp = dout("ndp", [nps, 8, 128, 128])
    nks = dout("nks", [128, 256]); nvs = dout("nvs", [128, 256])
    ncs = dout("ncs", [72, 128]); nds = dout("nds", [8, 128, 128])
    wscr = nc.dram_tensor("wscr", [NG, 128, 8, 512], BF16).ap()
    dgscr = nc.dram_tensor("dgscr", [8, 128, 12, 128], BF16).ap()

    with ExitStack() as st:
        S = Sched(nc, st)

        def sb(name, shape, dt=F32):
            return st.enter_context(nc.sbuf_tensor(name, list(shape), dt))

        def psum(name, shape, dt=F32):
            return st.enter_context(nc.psum_tensor(name, list(shape), dt))

        pmb = [psum(f"pm{i}", [128, 512]) for i in range(2)]
        ptb = psum("ptb", [128, 1024], BF16)
        pO = psum("pO", [128, 512])
        pw = [psum(f"pw{i}", [128, 512]) for i in range(4)]
        rr = {"pm": 0, "pmlim": 2, "pw": 0, "s0": 0, "s1": 0}

        PMDEEP = [("pm", 0), ("pm", 1), ("pw", 0), ("pw", 1), ("pw", 2)]

        def next_pm():
            lst = PMDEEP[:rr["pmlim"]]
            i = rr["pm"] % len(lst); rr["pm"] = i + 1
            nm, j = lst[i]
            bank = pmb[j] if nm == "pm" else pw[j]
            return bank[:, 0:256], (nm, j)

        def next_pm_full():
            lst = PMDEEP[:rr["pmlim"]]
            i = rr["pm"] % len(lst); rr["pm"] = i + 1
            nm, j = lst[i]
            return (pmb[j] if nm == "pm" else pw[j]), (nm, j)

        def next_pw():
            if rr["pmlim"] == 5:
                return pw[3], ("pw", 3)
            i = rr["pw"]; rr["pw"] = (i + 1) % 4
            return pw[i], ("pw", i)

        def next_ps(s):
            k = "s%d" % s
            i = 2 * s + rr[k]; rr[k] = 1 - rr[k]
            return pw[i], ("pw", i)

        ident_f = sb("ident_f", [128, 128]); ident_b = sb("ident_b", [128, 128], BF16)
        pswap_b = sb("pswap_b", [128, 128], BF16); pswap_f = sb("pswap_f", [128, 128])
        amask_f = sb("amask_f", [3, 640]); amask_b = sb("amask_b", [3, 640], BF16)
        dm = sb("dm", [128, 7, 128]); mpos4 = sb("mpos4s", [128, 512])
        ones_b = sb("ones_b", [128, 128], BF16)
        sinks = sb("sinks_s", [128, 16]); nsinks = sb("nsinks", [128, 16])
        convw = sb("convw_s", [128, 24, 4]); alog = sb("alog_s", [128, 8]); nega = sb("nega", [128, 8])
        dtb = sb("dtb_s", [128, 8]); normw = sb("normw_s", [128, 1]); nw2 = sb("nw2", [128, 1])
        lng = sb("lng_s", [128, D]); lnb = sb("lnb_s", [128, D])
        ropec = sb("ropec_s", [128, T]); ropes = sb("ropes_s", [128, T])
        TRI, BO, INDA, INDB, ONESF, OFFD, IDF = range(7)

        S.dma("sp", ident_f[:], ident_d, writes=["ident_f"])
        S.dma("sp", pswap_f[:], pswap_d, writes=["pswap_f"])
        S.dma("sp", amask_f[:], amask_d, writes=["amask_f"])
        S.dma("sp", dm[:], dmats_d, writes=["dm"])
        S.dma("sp", mpos4[:], mpos4_d, writes=["mpos4"])
        S.dma("sp", sinks[:], sinks_d.partition_broadcast(128), writes=["sinks"])
        S.dma("sp", convw[:], convw_d, writes=["convw"])
        S.dma("sp", alog[:], alog_d.partition_broadcast(128), writes=["alog"])
        S.dma("sp", dtb[:], dtb_d.partition_broadcast(128), writes=["dtb"])
        S.dma("sp", normw[:], normw_d, writes=["normw"])
        S.dma("sp", lng[:], lng_d.partition_broadcast(128), writes=["lng"])
        S.dma("sp", lnb[:], lnb_d.partition_broadcast(128), writes=["lnb"])
        S.op("dve", lambda e: e.tensor_copy(out=ident_b[:], in_=ident_f[:]), reads=["ident_f"], writes=["ident_b"])
        S.op("dve", lambda e: e.tensor_copy(out=pswap_b[:], in_=pswap_f[:]), reads=["pswap_f"], writes=["pswap_b"])
        S.op("dve", lambda e: e.tensor_copy(out=amask_b[:], in_=amask_f[:]), reads=["amask_f"], writes=["amask_b"])
        S.op("dve", lambda e: e.tensor_copy(out=ones_b[:], in_=dm[:, ONESF, :]), reads=["dm"], writes=["ones_b"])
        S.op("dve", lambda e: e.tensor_scalar(out=nsinks[:], in0=sinks[:], scalar1=-1.0, scalar2=None, op0=ALU.mult),
             reads=["sinks"], writes=["nsinks"])
        S.op("act", lambda e: e.activation(out=nega[:], in_=alog[:], func=AF.Exp), reads=["alog"], writes=["nega"])
        S.op("dve", lambda e: e.tensor_scalar(out=nega[:], in0=nega[:], scalar1=-1.0, scalar2=None, op0=ALU.mult),
             reads=["nega"], writes=["nega"])
        S.op("dve", lambda e: e.tensor_scalar(out=nw2[:], in0=normw[:], scalar1=float(np.sqrt(128.0)), scalar2=None, op0=ALU.mult),
             reads=["normw"], writes=["nw2"])

        xin = sb("xin", [128, D]); xbf = sb("xbf", [128, D], BF16)
        xT = sb("xT", [128, 8, T], BF16)
        QT = sb("QT", [128, 8, T], BF16)
        Kwin = sb("Kwin", [128, 4, 128 + T], BF16)
        kf = sb("kf", [128, 4, 128])
        Vwin = sb("Vwin", [128, 1 + T // 128, 4, 2, 128], BF16)
        vf = sb("vf", [128, 256])
        abt = sb("abt", [128, T // 128, 16])
        gtok = sb("gtok", [128, T // 128, 8]); btok = sb("btok", [128, T // 128, 8])
        tmpab = sb("tmpab", [128, T // 128, 8])
        bufZ = sb("bufZ", [128, 8, T], BF16); bufGate = sb("bufGate", [128, 8, T], BF16)
        bufG = sb("bufG", [128, 8, T], BF16); hA = sb("hA", [128, 8, T], BF16)
        tmpb = sb("tmpb", [128, T], BF16); tmpf1 = sb("tmpf1", [128, T]); tmpf2 = sb("tmpf2", [128, T])
        xres = sb("xres", [128, D]); yt = sb("yt", [128, D]); ysq = sb("ysq", [128, D])
        lnst = sb("lnst", [128, 8]); trf = sb("trf", [128, 128])
        ATT = []
        for s_ in range(2):
            ATT.append(dict(
                Pm=sb(f"Pm{s_}", [128, 4, 256], BF16), Pn=sb(f"Pn{s_}", [128, 4, 256], BF16), PT=sb(f"PT{s_}", [128, 8, 128], BF16),
                mx=sb(f"stmx{s_}", [128, 4]), ng=sb(f"stng{s_}", [128, 4]), sm=sb(f"stsm{s_}", [128, 4]),
                es=sb(f"stes{s_}", [128, 4]), rd=sb(f"strd{s_}", [128, 4])))
        ubs = [sb(f"ub{i}", [128, 3, 8 + T], BF16) for i in range(2)]
        dgb = [sb(f"dgb{i}", [128, 12, 128], BF16) for i in range(2)]
        dgbuild = dgb[0]
        hist = sb("hist", [128, 3, 24])
        rnf = sb("rnf", [128, 2, T])
        qTa = sb("qTa", [128, 8, T], BF16); kTa = sb("kTa", [128, 8, T], BF16); vTa = sb("vTa", [128, 8, T], BF16)
        Sf = sb("Sf", [128, 8, 128]); Sb = sb("Sb", [128, 8, 128], BF16)
        rhsb = sb("rhsb", [128, 8, 128], F32R)
        dmr = sb("dmr", [128, 2, 128], F32R); mposr = sb("mposr", [128, 512], F32R)
        dcT = sb("dcT", [128, 8, 128]); E1 = sb("E1", [128, 8, 128], BF16); bbs = sb("bbs", [128, 8, 128])
        gsm = sb("gsm", [128, 32]); egc = sb("egc", [128, 8]); edec = sb("edec", [128, 8]); ela = sb("ela", [128, 16])
        sc1 = sb("sc1", [128, 8]); ngc = sb("ngc", [128, 8])
        DEL = []
        for s_ in range(2):
            d_ = dict(
                kbT=sb(f"kbT{s_}", [128, 4, 128], BF16),
                Ab=[sb(f"Ab{s_}_{i}", [128, 4, 128], BF16) for i in range(2)],
                Nb=[sb(f"Nb{s_}_{i}", [128, 4, 128], BF16) for i in range(2)],
                Rf=sb(f"Rf{s_}", [128, 4, 128]), Rb=sb(f"Rb{s_}", [128, 4, 128], BF16),
                kbg=sb(f"kbg{s_}", [128, 4, 128], BF16), kdec=sb(f"kdec{s_}", [128, 4, 128], BF16),
                vbt=sb(f"vbt{s_}", [128, 4, 128], BF16), wTb=sb(f"wTb{s_}", [128, 4, 128], BF16),
                uf=sb(f"uf{s_}", [128, 4, 128]), qkT=sb(f"qkT{s_}", [128, 4, 128], BF16),
                qdT=sb(f"qdT{s_}", [128, 4, 128], BF16), vn=sb(f"vn{s_}", [128, 4, 128], BF16))
            d_["osq"] = d_["kbT"]
            d_["otmp"] = d_["kbg"]
            d_["rstd"] = d_["Rf"]
            d_["dsT"] = d_["uf"]
            DEL.append(d_)

        S.op("pool", lambda e: e.memset(Vwin[:], 0.0), writes=[("V", b) for b in range(1 + T // 128)])
        S.op("dve", lambda e: e.tensor_copy(out=dmr[:, 0, :], in_=dm[:, ONESF, :]), reads=["dm"], writes=["dmr"])
        S.op("dve", lambda e: e.tensor_copy(out=dmr[:, 1, :], in_=dm[:, IDF, :]), reads=["dm", "dmr"], writes=["dmr"])
        S.op("dve", lambda e: e.tensor_copy(out=mposr[:], in_=mpos4[:]), reads=["mpos4"], writes=["mposr"])

        for h in range(8):
            for part in range(3):
                cidx = part * 8 + h
                for j in range(4):
                    S.op("dve", lambda e: e.tensor_scalar(out=dgbuild[:, part * 4 + j, :], in0=ident_f[:], scalar1=convw[:, cidx, j:j + 1],
                                                          scalar2=None, op0=ALU.mult),
                         reads=["ident_f", "convw"], writes=[("dgb", 0)])
            S.dma("sp", dgscr[h], dgbuild[:], reads=[("dgb", 0)], writes=[("dgscr", h)])

        NWB = 3
        wbuf = [sb(f"wbuf{i}", [128, 8, 512], BF16) for i in range(NWB)]
        wstate = {"n": 0, "conv": set()}

        def wget(g):
            i = wstate["n"] % NWB
            wstate["n"] += 1
            wb = wbuf[i]; key = ("wbuf", i)
            nco = GCOLS[g]
            if g not in wstate["conv"]:
                wstate["conv"].add(g)
                for qd in range(4):
                    stg, sk = (yt, "yt") if qd % 2 == 0 else (ysq, "ysq")
                    sv = stg[:, :].rearrange("p (k c) -> p k c", k=2)
                    S.dma("sp", sv[:, :, 0:nco], wall[g, :, 2 * qd:2 * qd + 2, 0:nco], writes=[sk])
                    if qd % 2 == 0:
                        S.op("dve", lambda e: e.tensor_copy(out=wb[:, 2 * qd:2 * qd + 2, 0:nco], in_=sv[:, :, 0:nco]), reads=[sk], writes=[key])
                    else:
                        S.op("act", lambda e: e.copy(out=wb[:, 2 * qd:2 * qd + 2, 0:nco], in_=sv[:, :, 0:nco]), reads=[sk], writes=[key])
                S.dma("pool", wscr[g, :, :, 0:nco], wb[:, :, 0:nco], reads=[key], writes=[("wscr", g)])
            else:
                S.dma("sp", wb[:, :, 0:nco], wscr[g, :, :, 0:nco], reads=[("wscr", g)], writes=[key])
            return wb, key

        def v4(ap2d, h=4):
            return ap2d.rearrange("p (h n) -> p h n", h=h)

        def interleave(gens):
            gens = [g for g in gens if g is not None]
            while gens:
                alive = []
                for g in gens:
                    try:
                        next(g)
                        alive.append(g)
                    except StopIteration:
                        pass
                gens = alive

        def run_tile(kind, si, t0, Tn, first, last):
            bs = 128 if Tn >= 128 else Tn
            nblk = Tn // bs
            nch = bs // 64
            nk = 128 + bs
            xsrc = xp[si] if kind == "p" else xs
            ydst = yp[si] if kind == "p" else ys
            rp0 = t0 if kind == "p" else SEQ
            xkeys = [("xT", b) for b in range(nblk)]
            rr["pmlim"] = 2

            def proj(wb, wkey, c):
                pb, pk = next_pm()
                def f(e):
                    for k in range(8):
                        ins = e.matmul(out=pb[:, 0:Tn], lhsT=wb[:, k, c * 128:(c + 1) * 128], rhs=xT[:, k, 0:Tn],
                                       start=(k == 0), stop=(k == 7))
                    return ins
                S.op("pe", f, reads=[wkey] + xkeys, writes=[pk])
                return pb, pk

            def rope(pb, pk, out_ap, out_key, f32=None):
                S.op("act", lambda e: e.copy(out=tmpb[:, 0:Tn], in_=pb[:, 0:Tn]), reads=[pk], writes=["tmpb"])
                pwb, pwk = next_pw()
                S.op("pe", lambda e: e.matmul(out=pwb[:, 0:Tn], lhsT=pswap_b[:], rhs=tmpb[:, 0:Tn], start=True, stop=True),
                     reads=["tmpb", "pswap_b"], writes=[pwk])
                S.op("dve", lambda e: e.tensor_tensor(out=tmpf1[:, 0:Tn], in0=pwb[:, 0:Tn], in1=ropes[:, 0:Tn], op=ALU.mult),
                     reads=[pwk, "ropes"], writes=["tmpf1"])
                S.op("dve", lambda e: e.tensor_tensor(out=tmpf2[:, 0:Tn], in0=pb[:, 0:Tn], in1=ropec[:, 0:Tn], op=ALU.mult),
                     reads=[pk, "ropec"], writes=["tmpf2"])
                S.op("dve", lambda e: e.tensor_tensor(out=out_ap, in0=tmpf1[:, 0:Tn], in1=tmpf2[:, 0:Tn], op=ALU.add),
                     reads=["tmpf1", "tmpf2"], writes=[out_key])
                if f32 is not None:
                    fa, fk, nl = f32
                    S.op("pool", lambda e: e.tensor_tensor(out=fa, in0=tmpf1[:, Tn - nl:Tn], in1=tmpf2[:, Tn - nl:Tn], op=ALU.add),
                         reads=["tmpf1", "tmpf2"], writes=[fk])

            for b in range(nblk):
                r0 = t0 + b * bs
                S.dma("sp", xin[0:bs, :], xsrc[r0:r0 + bs, :], writes=["xin"])
                if b % 2 == 0:
                    S.op("dve", lambda e: e.tensor_copy(out=xbf[0:bs, :], in_=xin[0:bs, :]), reads=["xin"], writes=["xbf"])
                else:
                    S.op("act", lambda e: e.copy(out=xbf[0:bs, :], in_=xin[0:bs, :]), reads=["xin"], writes=["xbf"])
                def tr(e):
                    for k in range(8):
                        ins = e.transpose(out=ptb[:, k * 128:k * 128 + bs], in_=xbf[0:bs, k * 128:(k + 1) * 128],
                                          identity=ident_b[0:bs, 0:bs])
                    return ins
                S.op("pe", tr, reads=["xbf", "ident_b"], writes=["ptb"])
                S.op("act", lambda e: e.copy(out=xT[:, :, b * 128:b * 128 + bs], in_=v4(ptb[:, :], 8)[:, :, 0:bs]),
                     reads=["ptb"], writes=[("xT", b)])
            S.dma("sp", ropec[:, 0:Tn], ropec_d[:, rp0:rp0 + Tn], writes=["ropec"])
            S.dma("sp", ropes[:, 0:Tn], ropes_d[:, rp0:rp0 + Tn], writes=["ropes"])
            if DBG["stage"] < 1:
                return

            if first:
                if kind == "p":
                    S.op("pool", lambda e: e.memset(Kwin[:, :, 0:128], 0.0), writes=["Kprev"])
                    S.op("pool", lambda e: e.memset(Vwin[:, 0, :, :, :], 0.0), writes=[("V", 0)])
                    S.op("pool", lambda e: e.memset(hist[:], 0.0), writes=["hist"])
                    S.op("pool", lambda e: e.memset(Sf[:], 0.0), writes=[("Sf", h) for h in range(8)])
                    S.op("pool", lambda e: e.memset(Sb[:], 0.0), writes=[("Sb", 0), ("Sb", 1)])
                else:
                    S.dma("sp", xres[:, 0:512], ckd, writes=["xres"])
                    for g in range(4):
                        pwb, pwk = next_pw()
                        S.op("pe", lambda e: e.transpose(out=pwb[:, 0:128], in_=xres[:, g * 128:(g + 1) * 128], identity=ident_f[:]),
                             reads=["xres", "ident_f"], writes=[pwk])
                        S.op("act", lambda e: e.copy(out=Kwin[:, g, 0:128], in_=pwb[:, 0:128]), reads=[pwk], writes=["Kprev"])
                    S.dma("sp", vf[:], cv, writes=["vf"])
                    S.op("act", lambda e: e.copy(out=Vwin[:, 0, :, 0, 0:64], in_=v4(vf[:, :])), reads=["vf"], writes=[("V", 0)])
                    S.op("dve", lambda e: e.tensor_copy(out=Vwin[:, 0, :, 1, 64:128], in_=v4(vf[:, :])), reads=["vf", ("V", 0)], writes=[("V", 0)])
                    S.dma("sp", hist[:], sconv, writes=["hist"])
                    S.dma("sp", Sf[:], sdelta.rearrange("h d e -> d h e"), writes=[("Sf", h) for h in range(8)])
                    S.op("act", lambda e: e.copy(out=Sb[:], in_=Sf[:]), reads=[("Sf", h) for h in range(8)], writes=[("Sb", 0), ("Sb", 1)])
                    S.dma("pool", nks[0:64, :], ck[64:128, :])
                    S.dma("pool", nvs[0:64, :], cv[64:128, :])

            nl = min(128, Tn)
            wb, wk = wget(G_K)
            for g in range(4):
                pb, pk = proj(wb, wk, g)
                rope(pb, pk, Kwin[:, g, 128:128 + Tn], ("Kcur", g), f32=(kf[:, g, 0:nl], ("kf", g), nl) if last else None)
            if DBG["stage"] < 2:
                return
            wb, wk = wget(G_VAB)
            for b in range(nblk):
                pb_, pk = next_pw()
                def fv(e):
                    for k in range(8):
                        ins = e.matmul(out=pb_[0:bs, 0:272], lhsT=xT[:, k, b * 128:b * 128 + bs], rhs=wb[:, k, 0:272],
                                       start=(k == 0), stop=(k == 7))
                    return ins
                S.op("pe", fv, reads=[wk, ("xT", b)], writes=[pk])
                S.op("act", lambda e: e.copy(out=Vwin[0:bs, 1 + b, :, 0, 0:64], in_=v4(pb_[0:bs, 0:256])), reads=[pk], writes=[("V", 1 + b)])
                S.op("dve", lambda e: e.tensor_copy(out=Vwin[0:bs, 1 + b, :, 1, 64:128], in_=v4(pb_[0:bs, 0:256])),
                     reads=[pk, ("V", 1 + b)], writes=[("V", 1 + b)])
                S.op("dve", lambda e: e.tensor_copy(out=abt[0:bs, b, :], in_=pb_[0:bs, 256:272]), reads=[pk], writes=["abt"])
                if last and b == nblk - 1:
                    S.op("act", lambda e: e.copy(out=vf[0:bs, :], in_=pb_[0:bs, 0:256]), reads=[pk], writes=["vf"])
                    if kind == "p":
                        S.dma("pool", nvp[si], vf[:, :], reads=["vf"])
                    else:
                        S.dma("pool", nvs[64:128, :], vf[0:64, :], reads=["vf"])
            nb_ = nblk
            S.op("dve", lambda e: e.tensor_tensor(out=tmpab[0:bs, 0:nb_, :], in0=abt[0:bs, 0:nb_, 0:8],
                                                  in1=dtb[0:bs, :].unsqueeze(1).to_broadcast([bs, nb_, 8]), op=ALU.add),
                 reads=["abt", "dtb"], writes=["tmpab"])
            S.op("act", lambda e: e.activation(out=tmpab[0:bs, 0:nb_, :], in_=tmpab[0:bs, 0:nb_, :], func=AF.Exp), reads=["tmpab"], writes=["tmpab"])
            S.op("act", lambda e: e.activation(out=tmpab[0:bs, 0:nb_, :], in_=tmpab[0:bs, 0:nb_, :], func=AF.Ln, bias=1.0, scale=1.0),
                 reads=["tmpab"], writes=["tmpab"])
            S.op("dve", lambda e: e.tensor_tensor(out=gtok[0:bs, 0:nb_, :], in0=tmpab[0:bs, 0:nb_, :],
                                                  in1=nega[0:bs, :].unsqueeze(1).to_broadcast([bs, nb_, 8]), op=ALU.mult),
                 reads=["tmpab", "nega"], writes=["gtok"])
            S.op("act", lambda e: e.activation(out=btok[0:bs, 0:nb_, :], in_=abt[0:bs, 0:nb_, 8:16], func=AF.Exp, scale=-1.0), reads=["abt"], writes=["btok"])
            S.op("dve", lambda e: e.tensor_scalar(out=btok[0:bs, 0:nb_, :], in0=btok[0:bs, 0:nb_, :], scalar1=1.0, scalar2=None, op0=ALU.add), reads=["btok"], writes=["btok"])
            S.op("dve", lambda e: e.reciprocal(out=btok[0:bs, 0:nb_, :], in_=btok[0:bs, 0:nb_, :]), reads=["btok"], writes=["btok"])
            if DBG["stage"] < 3:
                return
            for half, gq in enumerate((G_Q0, G_Q1)):
                wb, wk = wget(gq)
                for c in range(4):
                    i = half * 4 + c
                    pb, pk = proj(wb, wk, c)
                    rope(pb, pk, QT[:, i, 0:Tn], ("QT", i))
            rr["pmlim"] = 5
            for half, gz in enumerate((G_ZA0, G_ZA1)):
                wb, wk = wget(gz)
                for c in range(4):
                    i = half * 4 + c
                    pb, pk = proj(wb, wk, c)
                    S.op("act", lambda e: e.activation(out=bufZ[:, i, 0:Tn], in_=pb[:, 0:Tn], func=AF.Silu), reads=[pk], writes=[("bufZ", i)])
            if DBG["stage"] < 4:
                return

            def att_unit(s, b, g):
                A = ATT[s]
                firstblk = first and b == 0 and kind == "p"
                mcol = 128 + (256 if firstblk else 0)
                pa, pak = pw[2 * s], ("pw", 2 * s)
                pb2, pbk = pw[2 * s + 1], ("pw", 2 * s + 1)
                banks = [pa, pb2, pa, pb2]; bkeys = [pak, pbk, pak, pbk]
                for hh in range(4):
                    h = 4 * g + hh; i = h // 2; r0 = (h % 2) * 64
                    def fs(e):
                        o = banks[hh][:, (hh // 2) * 256:(hh // 2) * 256 + 256]
                        ins = e.matmul(out=o, lhsT=QT[r0:r0 + 64, i, b * 128:b * 128 + 128],
                                       rhs=Kwin[r0:r0 + 64, g, b * 128:b * 128 + 256], start=True, stop=(kind != "p"))
                        if kind == "p":
                            ins = e.matmul(out=o, lhsT=amask_b[0:3, 0:128], rhs=amask_b[0:3, mcol:mcol + 256], start=False, stop=True)
                        return ins
                    S.op("pe", fs, reads=[("QT", i), "amask_b", "Kprev", ("Kcur", g)], writes=[bkeys[hh]])
                yield
                for j, (bk, bkk) in enumerate(((pa, pak), (pb2, pbk))):
                    S.op("dve", lambda e: e.reduce_max(out=A["mx"][0:bs, j:4:2], in_=v4(bk[0:bs, :], 2)[:, :, 0:nk], axis=AX.X),
                         reads=[bkk], writes=[("mx", s, j)])
                S.op("dve", lambda e: e.scalar_tensor_tensor(out=A["ng"][0:bs, :], in0=A["mx"][0:bs, :], scalar=-0.125,
                                                             in1=nsinks[0:bs, 4 * g:4 * g + 4], op0=ALU.mult, op1=ALU.min),
                     reads=[("mx", s, 0), ("mx", s, 1), "nsinks"], writes=[("ng", s)])
                S.op("pool", lambda e: e.memset(A["sm"][:], 0.0), writes=[("sm", s)])
                yield
                for hh in range(4):
                    S.op("act", lambda e: e.activation(out=A["Pm"][0:bs, hh, 0:nk],
                                                       in_=banks[hh][0:bs, (hh // 2) * 256:(hh // 2) * 256 + nk],
                                                       func=AF.Exp, bias=A["ng"][0:bs, hh:hh + 1], scale=0.125,
                                                       accum_out=A["sm"][0:bs, hh:hh + 1]),
                         reads=[bkeys[hh], ("ng", s), ("sm", s)], writes=[("Pm", s, hh), ("sm", s)])
                S.op("dve", lambda e: e.tensor_tensor(out=A["es"][0:bs, :], in0=sinks[0:bs, 4 * g:4 * g + 4], in1=A["ng"][0:bs, :], op=ALU.add),
                     reads=["sinks", ("ng", s)], writes=[("es", s)])
                S.op("act", lambda e: e.activation(out=A["es"][0:bs, :], in_=A["es"][0:bs, :], func=AF.Exp), reads=[("es", s)], writes=[("es", s)])
                yield
                S.op("dve", lambda e: e.tensor_tensor(out=A["es"][0:bs, :], in0=A["es"][0:bs, :], in1=A["sm"][0:bs, :], op=ALU.add),
                     reads=[("es", s), ("sm", s)], writes=[("es", s)])
                S.op("dve", lambda e: e.reciprocal(out=A["rd"][0:bs, :], in_=A["es"][0:bs, :]), reads=[("es", s)], writes=[("rd", s)])
                S.op("dve", lambda e: e.tensor_tensor(out=A["Pn"][0:bs, :, 0:nk], in0=A["Pm"][0:bs, :, 0:nk],
                                                      in1=A["rd"][0:bs, :].unsqueeze(2).to_broadcast([bs, 4, nk]), op=ALU.mult),
                     reads=[("Pm", s, hh) for hh in range(4)] + [("rd", s)], writes=[("Pn", s)])
                yield
                def ftr(e):
                    for hh in range(4):
                        ins = e.transpose(out=ptb[:, (2 * hh) * 128:(2 * hh) * 128 + bs], in_=A["Pn"][0:bs, hh, 0:128], identity=ident_b[0:bs, 0:bs])
                        ins = e.transpose(out=ptb[0:bs, (2 * hh + 1) * 128:(2 * hh + 1) * 128 + bs], in_=A["Pn"][0:bs, hh, 128:128 + bs],
                                          identity=ident_b[0:bs, 0:bs])
                    return ins
                S.op("pe", ftr, reads=[("Pn", s), "ident_b"], writes=["ptb"])
                S.op("act", lambda e: e.copy(out=A["PT"][:, :, 0:bs], in_=v4(ptb[:, :], 8)[:, :, 0:bs]), reads=["ptb"], writes=[("PT", s)])
                yield
                po = (pO if s == 0 else pmb[1])[:, 0:256]
                pok = "pO" if s == 0 else ("pm", 1)
                def fpv(e):
                    for c2 in range(2):
                        for s2 in range(2):
                            hh = 2 * c2 + s2
                            o = po[:, c2 * 128:c2 * 128 + bs]
                            e.matmul(out=o, lhsT=Vwin[:, b, g, s2, :], rhs=A["PT"][:, 2 * hh, 0:bs], start=(s2 == 0), stop=False)
                            ins = e.matmul(out=o, lhsT=Vwin[0:bs, b + 1, g, s2, :], rhs=A["PT"][0:bs, 2 * hh + 1, 0:bs],
                                           start=False, stop=(s2 == 1))
                    return ins
                S.op("pe", fpv, reads=[("PT", s), ("V", b), ("V", b + 1)], writes=[pok])
                yield
                S.op("dve", lambda e: e.tensor_tensor(out=bufG[:, 2 * g:2 * g + 2, b * 128:b * 128 + bs], in0=v4(po, 2)[:, :, 0:bs],
                                                      in1=bufZ[:, 2 * g:2 * g + 2, b * 128:b * 128 + bs], op=ALU.mult),
                     reads=[pok, ("bufZ", 2 * g), ("bufZ", 2 * g + 1)], writes=[("bufG", 2 * g, b), ("bufG", 2 * g + 1, b)])
                yield

            def att_stream(s):
                units = [(b, g) for b in range(nblk) for g in range(4)]
                for (b, g) in units[s::2]:
                    yield from att_unit(s, b, g)

            def conv_head(h):
                wb, wk = wget(G_D0 + h)
                dg = dgb[h % 2]; dgk = ("dgb", h % 2)
                ub = ubs[h % 2]; ubn = "ub%d" % (h % 2)
                S.dma("sp", dg[:], dgscr[h], reads=[("dgscr", h)], writes=[dgk])
                if DBG.get("conv", 9) < 1:
                    return
                for part in range(3):
                    cidx = part * 8 + h
                    pb, pk = proj(wb, wk, part)
                    if DBG.get("conv", 9) == 1:
                        cx = DBG.get("convx", 0)
                        if cx == 1:
                            S.op("dve", lambda e: e.tensor_copy(out=hist[:, :, cidx], in_=pb[:, Tn - 3:Tn]), reads=[pk, "hist"], writes=["hist"])
                        if cx == 4:
                            S.op("act", lambda e: e.copy(out=hist[:, :, cidx], in_=pb[:, Tn - 3:Tn]), reads=[pk, "hist"], writes=["hist"])
                        if cx == 2:
                            S.op("dve", lambda e: e.tensor_copy(out=ub[:, part, 0:3], in_=hist[:, :, cidx]), reads=["hist"], writes=[(ubn, part)])
                        if cx == 3:
                            S.op("act", lambda e: e.copy(out=ub[:, part, 3:3 + Tn], in_=pb[:, 0:Tn]), reads=[pk, (ubn, part)], writes=[(ubn, part)])
                        else:
                            S.op("act", lambda e: e.copy(out=ub[:, part, 4:4 + Tn], in_=pb[:, 0:Tn]), reads=[pk, (ubn, part)], writes=[(ubn, part)])
                        yield
                        continue
                    S.op("dve", lambda e: e.tensor_copy(out=ub[:, part, 0:3], in_=hist[:, :, cidx]), reads=["hist"], writes=[(ubn, part)])
                    S.op("act", lambda e: e.copy(out=hist[:, :, cidx], in_=pb[:, Tn - 3:Tn]), reads=[pk, "hist"], writes=["hist"])
                    S.op("act", lambda e: e.copy(out=ub[:, part, 3:3 + Tn], in_=pb[:, 0:Tn]), reads=[pk, (ubn, part)], writes=[(ubn, part)])
                    yield
                if DBG.get("conv", 9) < 2:
                    return
                for part in range(3):
                    pc, pck = next_pm()
                    def fc(e):
                        for j in range(4):
                            ins = e.matmul(out=pc[:, 0:Tn], lhsT=dg[:, part * 4 + j, :], rhs=ub[:, part, j:j + Tn], start=(j == 0), stop=(j == 3))
                        return ins
                    S.op("pe", fc, reads=[dgk, (ubn, part)], writes=[pck])
                    dst, nm = ((qTa, "qTa"), (kTa, "kTa"), (vTa, "vTa"))[part]
                    S.op("act", lambda e: e.activation(out=dst[:, h, 0:Tn], in_=pc[:, 0:Tn], func=AF.Silu), reads=[pck], writes=[(nm, h)])
                    yield

            def filler_att():
                for half, gg in enumerate((G_GA0, G_GA1)):
                    wb, wk = wget(gg)
                    for c in range(4):
                        i = half * 4 + c
                        pb, pk = proj(wb, wk, c)
                        S.op("act", lambda e: e.activation(out=bufGate[:, i, 0:Tn], in_=pb[:, 0:Tn], func=AF.Sigmoid), reads=[pk], writes=[("bufGate", i)])
                        yield
                if DBG["stage"] >= 6:
                    for h in range(8):
                        yield from conv_head(h)

            rr["pmlim"] = 1
            interleave([att_stream(0), att_stream(1)])
            rr["pmlim"] = 5
            interleave([filler_att()])
            if not last:
                S.op("pool", lambda e: e.tensor_copy(out=Kwin[:, :, 0:128], in_=Kwin[:, :, Tn:Tn + 128]),
                     reads=[("Kcur", g) for g in range(4)] + ["Kprev"], writes=["Kprev"])
                S.op("pool", lambda e: e.tensor_copy(out=Vwin[:, 0, :, :, :], in_=Vwin[:, nblk, :, :, :]),
                     reads=[("V", nblk), ("V", 0)], writes=[("V", 0)])
            if DBG["stage"] < 5:
                return
            if last:
                for g in range(4):
                    pwb, pwk = next_pw()
                    S.op("pe", lambda e: e.transpose(out=pwb[0:nl, 0:128], in_=kf[:, g, 0:nl], identity=ident_f[:]),
                         reads=[("kf", g), "ident_f"], writes=[pwk])
                    S.op("dve", lambda e: e.tensor_copy(out=yt[0:nl, g * 64:(g + 1) * 64], in_=pwb[0:nl, 0:64]), reads=[pwk], writes=["yt"])
                if kind == "p":
                    S.dma("pool", nkp[si], yt[:, 0:256], reads=["yt"])
                else:
                    S.dma("pool", nks[64:128, :], yt[0:64, 0:256], reads=["yt"])
            gkeys = [("bufG", ec, b) for ec in range(8) for b in range(nblk)]
            for half, go in enumerate((G_OA0, G_OA1)):
                wb, wk = wget(go)
                for c in range(4):
                    dd = half * 4 + c
                    pb, pk = next_pm()
                    def fo(e):
                        for ec in range(8):
                            ins = e.matmul(out=pb[:, 0:Tn], lhsT=wb[:, ec, c * 128:(c + 1) * 128], rhs=bufG[:, ec, 0:Tn], start=(ec == 0), stop=(ec == 7))
                        return ins
                    S.op("pe", fo, reads=[wk] + gkeys, writes=[pk])
                    S.op("dve", lambda e: e.tensor_tensor(out=hA[:, dd, 0:Tn], in0=pb[:, 0:Tn], in1=bufGate[:, dd, 0:Tn], op=ALU.mult),
                         reads=[pk, ("bufGate", dd)], writes=[("hA", dd)])
            if DBG["stage"] < 6 or DBG.get("conv", 9) < 3:
                return
            if last:
                pwb, pwk = next_pw()
                S.op("pe", lambda e: e.transpose(out=pwb[0:72, 0:128], in_=hist[:, :, :].rearrange("p t c -> p (t c)"), identity=ident_f[:]),
                     reads=["hist", "ident_f"], writes=[pwk])
                S.op("dve", lambda e: e.tensor_copy(out=trf[0:72, :], in_=pwb[0:72, 0:128]), reads=[pwk], writes=["trf"])
                S.dma("pool", ncp[si] if kind == "p" else ncs, trf[0:72, :], reads=["trf"])
            for which, (dst, sqbuf) in enumerate(((qTa, bufG), (kTa, bufGate))):
                nm = "qTa" if which == 0 else "kTa"
                allk = [(nm, h) for h in range(8)]
                bk_ = gkeys if which == 0 else [("bufGate", dd) for dd in range(8)]
                S.op("act", lambda e: e.activation(out=sqbuf[:, :, 0:Tn], in_=dst[:, :, 0:Tn], func=AF.Square), reads=allk, writes=bk_)
                for hp in range(4):
                    pwb, pwk = next_pm_full()
                    S.op("pe", lambda e: e.matmul(out=v4(pwb[:, :], 2)[:, :, 0:Tn], lhsT=ones_b[:], rhs=sqbuf[:, 2 * hp:2 * hp + 2, 0:Tn], start=True, stop=True),
                         reads=bk_ + ["ones_b"], writes=[pwk])
                    S.op("act", lambda e: e.activation(out=rnf[:, :, 0:Tn], in_=v4(pwb[:, :], 2)[:, :, 0:Tn], func=AF.Ln, bias=1e-6, scale=1.0),
                         reads=[pwk], writes=["rnf"])
                    S.op("act", lambda e: e.activation(out=rnf[:, :, 0:Tn], in_=rnf[:, :, 0:Tn], func=AF.Exp, scale=-0.5), reads=["rnf"], writes=["rnf"])
                    hk = [(nm, 2 * hp), (nm, 2 * hp + 1)]
                    if which == 0:
                        S.op("dve", lambda e: e.scalar_tensor_tensor(out=dst[:, 2 * hp:2 * hp + 2, 0:Tn], in0=dst[:, 2 * hp:2 * hp + 2, 0:Tn], scalar=float(128.0 ** -0.5),
                                                                     in1=rnf[:, :, 0:Tn], op0=ALU.mult, op1=ALU.mult),
                             reads=["rnf"] + hk, writes=hk)
                    else:
                        S.op("dve", lambda e: e.tensor_tensor(out=dst[:, 2 * hp:2 * hp + 2, 0:Tn], in0=dst[:, 2 * hp:2 * hp + 2, 0:Tn], in1=rnf[:, :, 0:Tn], op=ALU.mult),
                             reads=["rnf"] + hk, writes=hk)
            if DBG["stage"] < 7:
                return
            for half, gz in enumerate((G_ZB0, G_ZB1)):
                wb, wk = wget(gz)
                for c in range(4):
                    i = half * 4 + c
                    pb, pk = proj(wb, wk, c)
                    S.op("act", lambda e: e.activation(out=bufZ[:, i, 0:Tn], in_=pb[:, 0:Tn], func=AF.Silu), reads=[pk], writes=[("bufZ", i)])
            if DBG["stage"] < 8:
                return

            def delta_unit(s, b, hg):
                Dd = DEL[s]
                blk = slice(b * 128, b * 128 + bs)
                hs = slice(4 * hg, 4 * hg + 4)
                hl = list(range(4 * hg, 4 * hg + 4))
                K = lambda n: (n, s)
                Ab, Nb, Rf, Rb = Dd["Ab"], Dd["Nb"], Dd["Rf"], Dd["Rb"]
                pod = pO if s == 0 else pmb[1]
                podk = ["pO"] if s == 0 else [("pm", 1)]
                S.op("dve", lambda e: e.tensor_tensor(out=Dd["kbT"][:, :, 0:bs], in0=kTa[:, hs, blk], in1=bbs[:, hs, 0:bs], op=ALU.mult),
                     reads=[("kTa", h) for h in hl] + [("bbs", hg)], writes=[K("kbT")])
                S.op("pool", lambda e: e.tensor_tensor(out=Dd["dsT"][0:bs, :, 0:bs], in0=dcT[0:bs, hs, 0:bs],
                                                       in1=dm[0:bs, OFFD, 0:bs].unsqueeze(1).to_broadcast([bs, 4, bs]), op=ALU.mult),
                     reads=[("dcT", h) for h in hl] + ["dm"], writes=[K("uf")])
                pa_, pak = next_ps(s)
                def fkk(e):
                    for hq, h in enumerate(hl):
                        ins = e.matmul(out=pa_[0:bs, hq * 128:hq * 128 + bs], lhsT=kTa[:, h, blk], rhs=Dd["kbT"][:, hq, 0:bs], start=True, stop=True)
                    return ins
                S.op("pe", fkk, reads=[("kTa", h) for h in hl] + [K("kbT")], writes=[pak])
                yield
                S.op("dve", lambda e: e.scalar_tensor_tensor(out=Ab[0][0:bs, :, 0:bs], in0=v4(pa_[0:bs, :])[:, :, 0:bs], scalar=-1.0,
                                                             in1=Dd["dsT"][0:bs, :, 0:bs], op0=ALU.mult, op1=ALU.mult),
                     reads=[pak, K("uf")], writes=[K("Ab0")])
                def ftA(e):
                    for hq in range(4):
                        ins = e.transpose(out=ptb[0:bs, hq * 128:hq * 128 + bs], in_=Ab[0][0:bs, hq, 0:bs], identity=ident_b[0:bs, 0:bs])
                    return ins
                S.op("pe", ftA, reads=[K("Ab0"), "ident_b"], writes=["ptb"])
                S.op("act", lambda e: e.copy(out=Nb[0][0:bs, :, 0:bs], in_=v4(ptb[0:bs, 0:512])[:, :, 0:bs]), reads=["ptb"], writes=[K("Nb0")])
                S.op("pool", lambda e: e.tensor_tensor(out=Rf[0:bs, :, 0:bs], in0=Ab[0][0:bs, :, 0:bs],
                                                       in1=dm[0:bs, IDF, 0:bs].unsqueeze(1).to_broadcast([bs, 4, bs]), op=ALU.add),
                     reads=[K("Ab0"), "dm"], writes=[K("Rf")])
                S.op("act", lambda e: e.copy(out=Rb[0:bs, :, 0:bs], in_=Rf[0:bs, :, 0:bs]), reads=[K("Rf")], writes=[K("Rb")])
                yield
                cur = 0
                for lev in range(5):
                    nxt = 1 - cur
                    ck_, nk_ = K("Ab%d" % cur), K("Nb%d" % cur)
                    pn_, pnk = next_ps(s)
                    def fN(e):
                        for hq in range(4):
                            ins = e.matmul(out=pn_[0:bs, hq * 128:hq * 128 + bs], lhsT=Ab[cur][0:bs, hq, 0:bs], rhs=Nb[cur][0:bs, hq, 0:bs], start=True, stop=True)
                        return ins
                    S.op("pe", fN, reads=[ck_, nk_], writes=[pnk])
                    if lev < 4:
                        pa2, pa2k = next_ps(s)
                        def fA(e):
                            for hq in range(4):
                                ins = e.matmul(out=pa2[0:bs, hq * 128:hq * 128 + bs], lhsT=Nb[cur][0:bs, hq, 0:bs], rhs=Ab[cur][0:bs, hq, 0:bs], start=True, stop=True)
                            return ins
                        S.op("pe", fA, reads=[ck_, nk_], writes=[pa2k])
                    yield
                    S.op("dve", lambda e: e.tensor_copy(out=Nb[nxt][0:bs, :, 0:bs], in_=v4(pn_[0:bs, :])[:, :, 0:bs]), reads=[pnk], writes=[K("Nb%d" % nxt)])
                    if lev < 4:
                        S.op("act", lambda e: e.copy(out=Ab[nxt][0:bs, :, 0:bs], in_=v4(pa2[0:bs, :])[:, :, 0:bs]), reads=[pa2k], writes=[K("Ab%d" % nxt)])
                    pr_, prk = next_ps(s)
                    def fR(e):
                        for hq in range(4):
                            ins = e.matmul(out=pr_[0:bs, hq * 128:hq * 128 + bs], lhsT=Nb[nxt][0:bs, hq, 0:bs], rhs=Rb[0:bs, hq, 0:bs], start=True, stop=True)
                        return ins
                    S.op("pe", fR, reads=[K("Nb%d" % nxt), K("Rb")], writes=[prk])
                    yield
                    S.op("dve", lambda e: e.tensor_tensor(out=Rf[0:bs, :, 0:bs], in0=Rf[0:bs, :, 0:bs], in1=v4(pr_[0:bs, :])[:, :, 0:bs], op=ALU.add),
                         reads=[prk, K("Rf")], writes=[K("Rf")])
                    S.op("act", lambda e: e.copy(out=Rb[0:bs, :, 0:bs], in_=Rf[0:bs, :, 0:bs]), reads=[K("Rf")], writes=[K("Rb")])
                    cur = nxt
                def ftk(e):
                    for hq, h in enumerate(hl):
                        e.transpose(out=ptb[0:bs, hq * 128:(hq + 1) * 128], in_=kTa[:, h, blk], identity=ident_b[:])
                        ins = e.transpose(out=ptb[0:bs, 512 + hq * 128:512 + (hq + 1) * 128], in_=vTa[:, h, blk], identity=ident_b[:])
                    return ins
                S.op("pe", ftk, reads=[("kTa", h) for h in hl] + [("vTa", h) for h in hl] + ["ident_b"], writes=["ptb"])
                S.op("dve", lambda e: e.tensor_tensor(out=Dd["kbg"][0:bs, :, :], in0=v4(ptb[0:bs, 0:512]),
                                                      in1=sc1[0:bs, hs].unsqueeze(2).to_broadcast([bs, 4, 128]), op=ALU.mult),
                     reads=["ptb", "sc1"], writes=[K("kbg")])
                S.op("dve", lambda e: e.tensor_tensor(out=Dd["kdec"][0:bs, :, :], in0=v4(ptb[0:bs, 0:512]),
                                                      in1=edec[0:bs, hs].unsqueeze(2).to_broadcast([bs, 4, 128]), op=ALU.mult),
                     reads=["ptb", "edec"], writes=[K("kdec")])
                S.op("dve", lambda e: e.tensor_tensor(out=Dd["vbt"][0:bs, :, :], in0=v4(ptb[0:bs, 512:1024]),
                                                      in1=btok[0:bs, b, hs].unsqueeze(2).to_broadcast([bs, 4, 128]), op=ALU.mult),
                     reads=["ptb", "btok"], writes=[K("vbt")])
                yield
                pw_, pwk_ = next_ps(s)
                def fw_(e):
                    for hq in range(4):
                        ins = e.matmul(out=pw_[:, hq * 128:hq * 128 + bs], lhsT=Dd["kbg"][0:bs, hq, :], rhs=Rb[0:bs, hq, 0:bs], start=True, stop=True)
                    return ins
                S.op("pe", fw_, reads=[K("kbg"), K("Rb")], writes=[pwk_])
                pu_, puk = next_ps(s)
                def fu(e):
                    for hq in range(4):
                        ins = e.matmul(out=pu_[0:bs, hq * 128:(hq + 1) * 128], lhsT=Rb[0:bs, hq, 0:bs], rhs=Dd["vbt"][0:bs, hq, :], start=True, stop=True)
                    return ins
                S.op("pe", fu, reads=[K("vbt"), K("Rb")], writes=[puk])
                yield
                S.op("act", lambda e: e.copy(out=Dd["wTb"][:, :, 0:bs], in_=v4(pw_[:, :])[:, :, 0:bs]), reads=[pwk_], writes=[K("wTb")])
                S.op("dve", lambda e: e.tensor_copy(out=Dd["uf"][0:bs, :, :], in_=v4(pu_[0:bs, :])), reads=[puk], writes=[K("uf")])
                pq_, pqk = next_ps(s)
                def fqk(e):
                    for hq, h in enumerate(hl):
                        ins = e.matmul(out=pq_[0:bs, hq * 128:hq * 128 + bs], lhsT=kTa[:, h, blk], rhs=qTa[:, h, blk], start=True, stop=True)
                    return ins
                S.op("pe", fqk, reads=[("kTa", h) for h in hl] + [("qTa", h) for h in hl], writes=[pqk])
                S.op("pool", lambda e: e.tensor_tensor(out=Dd["qdT"][:, :, 0:bs], in0=qTa[:, hs, blk], in1=E1[:, hs, 0:bs], op=ALU.mult),
                     reads=[("qTa", h) for h in hl] + [("E1", hg)], writes=[K("qdT")])
                yield
                S.op("dve", lambda e: e.tensor_tensor(out=Dd["qkT"][0:bs, :, 0:bs], in0=v4(pq_[0:bs, :])[:, :, 0:bs], in1=dcT[0:bs, hs, 0:bs], op=ALU.mult),
                     reads=[pqk] + [("dcT", h) for h in hl], writes=[K("qkT")])
                for c in range(nch):
                    r = slice(c * 64, (c + 1) * 64)
                    ps_, psk = next_ps(s)
                    def fws(e):
                        for hq, h in enumerate(hl):
                            ins = e.matmul(out=ps_[0:bs, hq * 128:(hq + 1) * 128], lhsT=Dd["wTb"][:, hq, 0:bs], rhs=Sb[:, h, :], start=True, stop=True)
                        return ins
                    S.op("pe", fws, reads=[K("wTb"), ("Sb", hg)], writes=[psk])
                    yield
                    S.op("dve", lambda e: e.tensor_tensor(out=Dd["vn"][r, :, :], in0=Dd["uf"][r, :, :], in1=v4(ps_[r, :]), op=ALU.subtract),
                         reads=[psk, K("uf")], writes=[K("vn")])
                    def fo_(e):
                        for hq, h in enumerate(hl):
                            o = pod[:, hq * 128 + c * 64:hq * 128 + (c + 1) * 64]
                            e.matmul(out=o, lhsT=Sb[:, h, :], rhs=Dd["qdT"][:, hq, r], start=True, stop=False)
                            ins = e.matmul(out=o, lhsT=Dd["vn"][r, hq, :], rhs=Dd["qkT"][r, hq, r], start=False, stop=True)
                        return ins
                    S.op("pe", fo_, reads=[("Sb", hg), K("qdT"), K("vn"), K("qkT")], writes=podk)
                    pS_, pSk = next_ps(s)
                    def fsu(e):
                        for hq in range(4):
                            ins = e.matmul(out=pS_[:, hq * 128:(hq + 1) * 128], lhsT=Dd["kdec"][r, hq, :], rhs=Dd["vn"][r, hq, :], start=True, stop=True)
                        return ins
                    S.op("pe", fsu, reads=[K("kdec"), K("vn")], writes=[pSk])
                    yield
                    for hq, h in enumerate(hl):
                        S.op("dve", lambda e: e.scalar_tensor_tensor(out=Sf[:, h, :], in0=Sf[:, h, :], scalar=ela[:, 8 * c + h:8 * c + h + 1],
                                                                     in1=pS_[:, hq * 128:(hq + 1) * 128], op0=ALU.mult, op1=ALU.add),
                             reads=[("Sf", h), "ela", pSk], writes=[("Sf", h)])
                    S.op("act", lambda e: e.copy(out=Sb[:, hs, :], in_=Sf[:, hs, :]), reads=[("Sf", h) for h in hl], writes=[("Sb", hg)])
                    yield
                S.op("act", lambda e: e.activation(out=Dd["osq"][:, :, 0:bs], in_=v4(pod[:, :])[:, :, 0:bs], func=AF.Square), reads=podk, writes=[K("kbT")])
                pss, pssk = next_ps(s)
                S.op("pe", lambda e: e.matmul(out=v4(pss[:, :])[:, :, 0:bs], lhsT=ones_b[:], rhs=Dd["osq"][:, :, 0:bs], start=True, stop=True),
                     reads=[K("kbT"), "ones_b"], writes=[pssk])
                yield
                S.op("act", lambda e: e.activation(out=Dd["rstd"][:, :, 0:bs], in_=v4(pss[:, :])[:, :, 0:bs], func=AF.Ln, bias=128.0 * 1e-6, scale=1.0),
                     reads=[pssk], writes=[K("Rf")])
                S.op("act", lambda e: e.activation(out=Dd["rstd"][:, :, 0:bs], in_=Dd["rstd"][:, :, 0:bs], func=AF.Exp, scale=-0.5),
                     reads=[K("Rf")], writes=[K("Rf")])
                S.op("dve", lambda e: e.tensor_tensor(out=Dd["otmp"][:, :, 0:bs], in0=v4(pod[:, :])[:, :, 0:bs], in1=Dd["rstd"][:, :, 0:bs], op=ALU.mult),
                     reads=podk + [K("Rf")], writes=[K("kbg")])
                S.op("dve", lambda e: e.scalar_tensor_tensor(out=bufG[:, hs, blk], in0=Dd["otmp"][:, :, 0:bs], scalar=nw2[:, 0:1],
                                                             in1=bufZ[:, hs, blk], op0=ALU.mult, op1=ALU.mult),
                     reads=[K("kbg"), "nw2"] + [("bufZ", h) for h in hl], writes=[("bufG", h, b) for h in hl])
                yield

            def gate_prep(b):
                g2 = gtok[0:bs, b, :]; b2 = btok[0:bs, b, :]
                pwb, pwk = next_pw()
                def fg(e):
                    e.matmul(out=pwb[0:bs, 0:8], lhsT=dm[0:bs, TRI, 0:bs], rhs=g2, start=True, stop=True)
                    e.matmul(out=pwb[0:bs, 8:16], lhsT=dm[0:bs, BO, 0:bs], rhs=g2, start=True, stop=True)
                    e.matmul(out=pwb[0:128, 16:24], lhsT=dm[0:bs, INDA, :], rhs=g2, start=True, stop=True)
                    return e.matmul(out=pwb[0:128, 24:32], lhsT=dm[0:bs, INDB, :], rhs=g2, start=True, stop=True)
                S.op("pe", fg, reads=["dm", "gtok"], writes=[pwk])
                S.op("dve", lambda e: e.tensor_copy(out=gsm[:, :], in_=pwb[:, 0:32]), reads=[pwk], writes=["gsm"])
                S.op("act", lambda e: e.activation(out=egc[0:bs, :], in_=gsm[0:bs, 0:8], func=AF.Exp), reads=["gsm"], writes=["egc"])
                S.op("dve", lambda e: e.tensor_scalar(out=ngc[0:bs, :], in0=gsm[0:bs, 0:8], scalar1=-1.0, scalar2=None, op0=ALU.mult), reads=["gsm"], writes=["ngc"])
                S.op("dve", lambda e: e.tensor_tensor(out=edec[0:bs, :], in0=gsm[0:bs, 8:16], in1=gsm[0:bs, 0:8], op=ALU.subtract), reads=["gsm"], writes=["edec"])
                S.op("act", lambda e: e.activation(out=edec[0:bs, :], in_=edec[0:bs, :], func=AF.Exp), reads=["edec"], writes=["edec"])
                S.op("act", lambda e: e.activation(out=ela[:, :], in_=gsm[:, 16:32], func=AF.Exp), reads=["gsm"], writes=["ela"])
                S.op("dve", lambda e: e.tensor_tensor(out=sc1[0:bs, :], in0=egc[0:bs, :], in1=b2, op=ALU.mult), reads=["egc", "btok"], writes=["sc1"])
                S.op("dve", lambda e: e.scalar_tensor_tensor(out=rhsb[0:bs, :, 0:bs], in0=g2.unsqueeze(2).to_broadcast([bs, 8, bs]), scalar=-1.0,
                                                             in1=dm[0:bs, TRI, 0:bs].unsqueeze(1).to_broadcast([bs, 8, bs]), op0=ALU.mult, op1=ALU.mult),
                     reads=["gtok", "dm"], writes=["rhsb"])
                for hf in range(2):
                    p1, p1k = next_pw()
                    S.op("pe", lambda e: e.matmul(out=v4(p1[:, :])[:, :, 0:bs], lhsT=dmr[0:bs, 0, :], rhs=rhsb[0:bs, 4 * hf:4 * hf + 4, 0:bs], start=True, stop=True),
                         reads=["dmr", "rhsb"], writes=[p1k])
                    S.op("act", lambda e: e.activation(out=E1[:, 4 * hf:4 * hf + 4, 0:bs], in_=v4(p1[:, :])[:, :, 0:bs], func=AF.Exp, scale=-1.0),
                         reads=[p1k], writes=[("E1", hf)])
                    p2, p2k = next_pw()
                    def fx(e):
                        o = v4(p2[0:bs, :])[:, :, 0:bs]
                        e.matmul(out=o, lhsT=dmr[0:bs, 0, 0:bs], rhs=rhsb[0:bs, 4 * hf:4 * hf + 4, 0:bs], start=True, stop=False)
                        return e.matmul(out=o, lhsT=dmr[0:bs, 1, 0:bs], rhs=v4(mposr[0:bs, :])[:, :, 0:bs], start=False, stop=True)
                    S.op("pe", fx, reads=["dmr", "rhsb", "mposr"], writes=[p2k])
                    for hq in range(4):
                        h = 4 * hf + hq
                        S.op("act", lambda e: e.activation(out=dcT[0:bs, h, 0:bs], in_=p2[0:bs, hq * 128:hq * 128 + bs], func=AF.Exp,
                                                           bias=ngc[0:bs, h:h + 1], scale=-1.0),
                             reads=[p2k, "ngc"], writes=[("dcT", h)])
                S.op("dve", lambda e: e.tensor_tensor(out=rhsb[0:bs, :, 0:bs], in0=b2.unsqueeze(2).to_broadcast([bs, 8, bs]),
                                                      in1=dm[0:bs, IDF, 0:bs].unsqueeze(1).to_broadcast([bs, 8, bs]), op=ALU.mult),
                     reads=["btok", "dm"], writes=["rhsb"])
                for hf in range(2):
                    p3, p3k = next_pw()
                    S.op("pe", lambda e: e.matmul(out=v4(p3[:, :])[:, :, 0:bs], lhsT=dmr[0:bs, 0, :], rhs=rhsb[0:bs, 4 * hf:4 * hf + 4, 0:bs], start=True, stop=True),
                         reads=["dmr", "rhsb"], writes=[p3k])
                    S.op("act", lambda e: e.copy(out=bbs[:, 4 * hf:4 * hf + 4, 0:bs], in_=v4(p3[:, :])[:, :, 0:bs]), reads=[p3k], writes=[("bbs", hf)])

            def filler_delta():
                for half, gg in enumerate((G_GB0, G_GB1)):
                    wb, wk = wget(gg)
                    for c in range(4):
                        i = half * 4 + c
                        pb, pk = proj(wb, wk, c)
                        S.op("act", lambda e: e.activation(out=bufGate[:, i, 0:Tn], in_=pb[:, 0:Tn], func=AF.Sigmoid), reads=[pk], writes=[("bufGate", i)])
                        yield

            interleave([filler_delta()])
            rr["pmlim"] = 1
            for b in range(nblk):
                gate_prep(b)
                interleave([delta_unit(0, b, 0), delta_unit(1, b, 1)])
            rr["pmlim"] = 5
            if last:
                S.dma("pool", (ndp[si] if kind == "p" else nds).rearrange("h d e -> d h e"), Sf[:, :, :], reads=[("Sf", h) for h in range(8)])
            if DBG["stage"] < 9:
                return

            for half, go in enumerate((G_OB0, G_OB1)):
                wb, wk = wget(go)
                for c in range(4):
                    dd = half * 4 + c
                    pb, pk = next_pm()
                    def fo2(e):
                        for ec in range(8):
                            ins = e.matmul(out=pb[:, 0:Tn], lhsT=wb[:, ec, c * 128:(c + 1) * 128], rhs=bufG[:, ec, 0:Tn], start=(ec == 0), stop=(ec == 7))
                        return ins
                    S.op("pe", fo2, reads=[wk] + gkeys, writes=[pk])
                    S.op("dve", lambda e: e.tensor_tensor(out=tmpf1[:, 0:Tn], in0=pb[:, 0:Tn], in1=bufGate[:, dd, 0:Tn], op=ALU.mult),
                         reads=[pk, ("bufGate", dd)], writes=["tmpf1"])
                    S.op("dve", lambda e: e.tensor_tensor(out=hA[:, dd, 0:Tn], in0=tmpf1[:, 0:Tn], in1=hA[:, dd, 0:Tn], op=ALU.add),
                         reads=["tmpf1", ("hA", dd)], writes=[("hA", dd)])
            if DBG["stage"] < 10:
                return
            wo0, wk0 = wget(G_OUT0)
            wo1, wk1 = wget(G_OUT1)
            hkeys = [("hA", dd) for dd in range(8)]
            for b in range(nblk):
                r0 = t0 + b * bs
                S.dma("sp", xres[0:bs, :], xsrc[r0:r0 + bs, :], writes=["xres"])
                S.op("pool", lambda e: e.memset(lnst[:], 0.0), writes=["lnst"])
                for half, (wo, wkk) in enumerate(((wo0, wk0), (wo1, wk1))):
                    pf, pfk = next_pm_full()
                    def ff(e):
                        for ec in range(8):
                            ins = e.matmul(out=pf[0:bs, :], lhsT=hA[:, ec, b * 128:b * 128 + bs], rhs=wo[:, ec, :], start=(ec == 0), stop=(ec == 7))
                        return ins
                    S.op("pe", ff, reads=[wkk] + hkeys, writes=[pfk])
                    S.op("dve", lambda e: e.scalar_tensor_tensor(out=yt[0:bs, half * 512:(half + 1) * 512], in0=xres[0:bs, half * 512:(half + 1) * 512],
                                                                 scalar=float(ALPHA), in1=pf[0:bs, :], op0=ALU.mult, op1=ALU.add),
                         reads=[pfk, "xres", "yt"], writes=["yt"])
                S.op("act", lambda e: e.activation(out=ysq[0:bs, :], in_=yt[0:bs, :], func=AF.Identity, accum_out=lnst[0:bs, 0:1]),
                     reads=["yt", "lnst"], writes=["ysq", "lnst"])
                S.op("act", lambda e: e.activation(out=ysq[0:bs, :], in_=yt[0:bs, :], func=AF.Square, accum_out=lnst[0:bs, 1:2]),
                     reads=["yt", "lnst", "ysq"], writes=["ysq", "lnst"])
                S.op("dve", lambda e: e.tensor_scalar(out=lnst[0:bs, 2:4], in0=lnst[0:bs, 0:2], scalar1=1.0 / D, scalar2=None, op0=ALU.mult),
                     reads=["lnst"], writes=["lnst"])
                S.op("dve", lambda e: e.tensor_tensor(out=lnst[0:bs, 4:5], in0=lnst[0:bs, 2:3], in1=lnst[0:bs, 2:3], op=ALU.mult), reads=["lnst"], writes=["lnst"])
                S.op("dve", lambda e: e.tensor_tensor(out=lnst[0:bs, 5:6], in0=lnst[0:bs, 3:4], in1=lnst[0:bs, 4:5], op=ALU.subtract), reads=["lnst"], writes=["lnst"])
                S.op("act", lambda e: e.activation(out=lnst[0:bs, 5:6], in_=lnst[0:bs, 5:6], func=AF.Ln, bias=1e-5, scale=1.0), reads=["lnst"], writes=["lnst"])
                S.op("act", lambda e: e.activation(out=lnst[0:bs, 6:7], in_=lnst[0:bs, 5:6], func=AF.Exp, scale=-0.5), reads=["lnst"], writes=["lnst"])
                S.op("dve", lambda e: e.tensor_scalar(out=ysq[0:bs, :], in0=yt[0:bs, :], scalar1=lnst[0:bs, 2:3], scalar2=lnst[0:bs, 6:7],
                                                      op0=ALU.subtract, op1=ALU.mult),
                     reads=["yt", "lnst", "ysq"], writes=["ysq"])
                S.op("dve", lambda e: e.tensor_tensor(out=ysq[0:bs, :], in0=ysq[0:bs, :], in1=lng[0:bs, :], op=ALU.mult), reads=["ysq", "lng"], writes=["ysq"])
                S.op("pool", lambda e: e.tensor_tensor(out=ysq[0:bs, :], in0=ysq[0:bs, :], in1=lnb[0:bs, :], op=ALU.add), reads=["ysq", "lnb"], writes=["ysq"])
                S.dma("pool", ydst[r0:r0 + bs, :], ysq[0:bs, :], reads=["ysq"])

        for si in range(0 if DBG.get("skip_prompt") else nps):
            nt = SEQ // T
            for ti in range(nt):
                run_tile("p", si, ti * T, T, ti == 0, ti == nt - 1)
        if with_sample:
            run_tile("s", 0, 0, DEC, True, True)
        print("[sched counts]", S.cnt, S.dma_cnt, flush=True)
        S.emit()
    return nc


_PROG = {}


def kernel(x_prompt, x_sample, cache_attn_k, cache_attn_v, state_conv, state_delta, w_in, attn_sinks, conv_w,
           a_log, dt_bias, delta_norm_w, w_o_attn, w_o_delta, w_out, ln_g, ln_b):
    f = lambda a: np.ascontiguousarray(np.asarray(a, dtype=np.float32))
    x_prompt = f(x_prompt); x_sample = f(x_sample)
    n = 8
    consts = _host_consts()
    wall = _wall(f(w_in), f(w_o_attn), f(w_o_delta), f(w_out))
    cw = f(conv_w)
    convw = np.ascontiguousarray(cw.reshape(4, 24, 128).transpose(2, 1, 0))
    common = dict(wall=wall, sinks=f(attn_sinks), convw=convw, alog=f(a_log), dtb=f(dt_bias),
                  normw=f(delta_norm_w).reshape(128, 1), lng=f(ln_g), lnb=f(ln_b), **consts)
    ck = f(cache_attn_k); cv = f(cache_attn_v); sc = f(state_conv); sd = f(state_delta)
    in_maps = []
    for c in range(n):
        ckc = ck[c].reshape(128, 4, 64)
        ckd = np.ascontiguousarray(np.stack([ckc, ckc], 2).reshape(128, 512))
        scv = np.ascontiguousarray(sc[c].reshape(3, 24, 128).transpose(2, 0, 1))
        in_maps.append(dict(xp=x_prompt[NPS * c:NPS * (c + 1)], xs=x_sample[c], ckd=ckd, ck=ck[c].reshape(128, 256),
                            cv=cv[c].reshape(128, 256), sconv=scv, sdelta=sd[c], **common))
    if "nc" not in _PROG:
        _PROG["nc"] = build_program()
    res = run_bass_kernel_spmd(_PROG["nc"], in_maps, core_ids=list(range(n)))
    R = res.results
    cat = lambda k: np.concatenate([R[c][k] for c in range(n)], 0)
    stk = lambda k: np.stack([R[c][k] for c in range(n)], 0)
    y_p = cat("yp")
    y_s = stk("ys")
    nk_p = cat("nkp").reshape(32, 128, 4, 64); nv_p = cat("nvp").reshape(32, 128, 4, 64)
    nc_p = cat("ncp").reshape(32, 3, 3072); nd_p = cat("ndp")
    nk_s = stk("nks").reshape(8, 128, 4, 64); nv_s = stk("nvs").reshape(8, 128, 4, 64)
    nc_s = stk("ncs").reshape(8, 3, 3072); nd_s = stk("nds")
    return (y_p, y_s, nk_p, nv_p, nc_p, nd_p, nk_s, nv_s, nc_s, nd_s)
```
